# Optimizing a Trainium2 kernel written in Bass

```python
import math
import jax, jax.numpy as jnp
from jax import lax
import numpy as np

D_MODEL = 2048
BATCH = 4
SEQ = 2048
DEPTH = 2

CHUNK = 64
Q_BLOCK = 128
NORM_EPS = 1e-6
DN_HEADS = 8
DN_HEAD_DIM = 128
DN_WIDTH = DN_HEADS * DN_HEAD_DIM
DN_CONV = 4
SG_GROUPS = 8
SG_GROUP_DIM = 128
SG_WIDTH = SG_GROUPS * SG_GROUP_DIM
SG_BLOCK = 128
MLA_HEADS = 8
MLA_Q_RANK = 512
MLA_KV_RANK = 512
MLA_NOPE = 128
MLA_ROPE = 64
MLA_V = 128
ROPE_THETA = 10000.0
N_BRANCH = 3
BRANCH_WIDTH = 1024
D_FF = 5632
IN_COLS = (4 * DN_WIDTH + 2 * DN_HEADS) + 2 * SG_WIDTH + (MLA_Q_RANK + MLA_KV_RANK + MLA_ROPE) + N_BRANCH * D_MODEL

kernel_name = "hybrid_deltanet_gmlp_mla_macaron"


def rms_norm(x, g, eps=NORM_EPS):
    xf = x.astype(jnp.float32)
    y = xf * lax.rsqrt(jnp.mean(xf * xf, axis=-1, keepdims=True) + eps)
    return (y * g.astype(jnp.float32)).astype(x.dtype)


def layer_norm(x, g, b, eps=1e-5):
    xf = x.astype(jnp.float32)
    mu = jnp.mean(xf, axis=-1, keepdims=True)
    var = jnp.mean(jnp.square(xf - mu), axis=-1, keepdims=True)
    y = (xf - mu) * lax.rsqrt(var + eps) * g.astype(jnp.float32) + b.astype(jnp.float32)
    return y.astype(x.dtype)


def l2_normalize(x, eps=1e-6):
    return x * lax.rsqrt(jnp.sum(x * x, axis=-1, keepdims=True) + eps)


def swiglu(h, w_gate, w_up, w_down):
    return (jax.nn.silu(h @ w_gate) * (h @ w_up)) @ w_down


def causal_conv(x, w):
    k_len, ch = w.shape
    return lax.conv_general_dilated(x, w[:, None, :], window_strides=(1,), padding=[(k_len - 1, 0)],
                                    dimension_numbers=("NWC", "WIO", "NWC"), feature_group_count=ch)


def apply_rope(x, cos, sin):
    xf = x.astype(jnp.float32)
    x1, x2 = jnp.split(xf, 2, axis=-1)
    return jnp.concatenate([x1 * cos - x2 * sin, x1 * sin + x2 * cos], axis=-1).astype(x.dtype)


def _offsets(sizes):
    out, acc = [], 0
    for s in sizes:
        acc += s
        out.append(acc)
    return out


def chunk_gated_delta(q, k, v, g, beta):
    B, S, H, DK = q.shape
    DV = v.shape[-1]
    N = S // CHUNK

    def to_chunks(t):
        return t.reshape((B, N, CHUNK) + t.shape[2:]).swapaxes(2, 3)

    q, k, v, g, beta = (to_chunks(t) for t in (q, k, v, g, beta))
    g = jnp.cumsum(g, axis=-1)
    idx = jnp.arange(CHUNK)
    incl = idx[:, None] >= idx[None, :]
    strict = idx[:, None] > idx[None, :]
    decay = jnp.exp(jnp.where(incl, g[..., :, None] - g[..., None, :], -jnp.inf))
    k_beta = k * beta[..., None]
    a_mat = jnp.where(strict, jnp.einsum("bnhid,bnhjd->bnhij", k_beta, k) * decay, 0.0)
    rhs = jnp.concatenate([k_beta * jnp.exp(g)[..., None], v * beta[..., None]], axis=-1)
    sol = lax.linalg.triangular_solve(a_mat + jnp.eye(CHUNK, dtype=a_mat.dtype), rhs,
                                      left_side=True, lower=True, unit_diagonal=True)
    w, u = sol[..., :DK], sol[..., DK:]
    qk = jnp.einsum("bnhid,bnhjd->bnhij", q, k) * decay

    def step(state, inp):
        q_c, k_c, u_c, w_c, g_c, qk_c = inp
        v_new = u_c - jnp.einsum("bhcd,bhde->bhce", w_c, state)
        o_c = (jnp.einsum("bhcd,bhde->bhce", q_c * jnp.exp(g_c)[..., None], state)
               + jnp.einsum("bhij,bhje->bhie", qk_c, v_new))
        g_last = g_c[..., -1:]
        state = (state * jnp.exp(g_last)[..., None]
                 + jnp.einsum("bhcd,bhce->bhde", k_c * jnp.exp(g_last - g_c)[..., None], v_new))
        return state, o_c

    state0 = jnp.zeros((B, H, DK, DV), q.dtype)
    xs = tuple(t.swapaxes(0, 1) for t in (q, k, u, w, g, qk))
    _, o = lax.scan(step, state0, xs)
    return o.transpose(1, 0, 3, 2, 4).reshape(B, S, H, DV)


def deltanet_branch(qkv, z, a, b, conv_w, a_log, dt_bias, norm_g):
    B, S, _ = qkv.shape
    f32 = jnp.float32
    qkv = jax.nn.silu(causal_conv(qkv, conv_w))
    q, k, v = [t.reshape(B, S, DN_HEADS, DN_HEAD_DIM).astype(f32) for t in jnp.split(qkv, 3, axis=-1)]
    q = l2_normalize(q) * (DN_HEAD_DIM ** -0.5)
    k = l2_normalize(k)
    beta = jax.nn.sigmoid(b.astype(f32))
    g = -jnp.exp(a_log.astype(f32)) * jax.nn.softplus(a.astype(f32) + dt_bias.astype(f32))
    o = chunk_gated_delta(q, k, v, g, beta)
    o = rms_norm(o, norm_g) * jax.nn.silu(z.reshape(B, S, DN_HEADS, DN_HEAD_DIM).astype(f32))
    return o.reshape(B, S, DN_WIDTH).astype(qkv.dtype)


def spatial_gating_branch(sg_in, ln_g, ln_b, w_s, b_s):
    B, S, _ = sg_in.shape
    zz = jax.nn.gelu(sg_in, approximate=False)
    u, v = jnp.split(zz, 2, axis=-1)
    v = layer_norm(v, ln_g, ln_b)
    v = v.reshape(B, S // SG_BLOCK, SG_BLOCK, SG_GROUPS, SG_GROUP_DIM)
    chunk_id = jnp.arange(SG_BLOCK) // CHUNK
    mask = chunk_id[None, :] <= chunk_id[:, None]
    w = jnp.where(mask[None], w_s, 0.0)
    mixed = jnp.einsum("gij,bnjgc->bnigc", w.astype(v.dtype), v) + b_s.T[:, :, None].astype(v.dtype)
    return u * mixed.reshape(B, S, SG_WIDTH)


def mla_branch(cq, ckv, kr, cos, sin, cq_g, ckv_g, w_uq, w_ukv):
    B, S, _ = cq.shape
    q = (rms_norm(cq, cq_g) @ w_uq).reshape(B, S, MLA_HEADS, MLA_NOPE + MLA_ROPE)
    q_nope = q[..., :MLA_NOPE]
    q_rope = apply_rope(q[..., MLA_NOPE:], cos[:, :, None], sin[:, :, None])
    kv = (rms_norm(ckv, ckv_g) @ w_ukv).reshape(B, S, MLA_HEADS, MLA_NOPE + MLA_V)
    k_nope, v = kv[..., :MLA_NOPE], kv[..., MLA_NOPE:]
    k_rope = apply_rope(kr, cos, sin)
    nb = S // Q_BLOCK
    qn_blocks = q_nope.reshape(B, nb, Q_BLOCK, MLA_HEADS, MLA_NOPE).swapaxes(0, 1)
    qr_blocks = q_rope.reshape(B, nb, Q_BLOCK, MLA_HEADS, MLA_ROPE).swapaxes(0, 1)
    key_chunk = jnp.arange(S) // CHUNK
    scale = (MLA_NOPE + MLA_ROPE) ** -0.5

    def attend(args):
        blk, qn, qr = args
        q_chunk = (blk * Q_BLOCK + jnp.arange(Q_BLOCK)) // CHUNK
        s = (jnp.einsum("bqhd,bkhd->bhqk", qn, k_nope, preferred_element_type=jnp.float32)
             + jnp.einsum("bqhd,bkd->bhqk", qr, k_rope, preferred_element_type=jnp.float32)) * scale
        s = jnp.where(key_chunk[None, :] <= q_chunk[:, None], s, -1e30)
        p = jax.nn.softmax(s, axis=-1)
        return jnp.einsum("bhqk,bkhd->bqhd", p.astype(v.dtype), v)

    o = lax.map(attend, (jnp.arange(nb), qn_blocks, qr_blocks))
    return o.swapaxes(0, 1).reshape(B, S, MLA_HEADS * MLA_V)


def hybrid_mixer(h, cos, sin, w_in, conv_w, a_log, dt_bias, dn_g, ln_g, ln_b, sg_w, sg_b,
                 cq_g, ckv_g, w_uq, w_ukv, w_branch, w_out):
    B, S, D = h.shape
    proj = h @ w_in
    sizes = [3 * DN_WIDTH, DN_WIDTH, DN_HEADS, DN_HEADS, 2 * SG_WIDTH, MLA_Q_RANK, MLA_KV_RANK, MLA_ROPE]
    qkv, z, a, b, sg_in, cq, ckv, kr, gate_logits = jnp.split(proj, _offsets(sizes), axis=-1)
    o_a = deltanet_branch(qkv, z, a, b, conv_w, a_log, dt_bias, dn_g)
    o_b = spatial_gating_branch(sg_in, ln_g, ln_b, sg_w, sg_b)
    o_c = mla_branch(cq, ckv, kr, cos, sin, cq_g, ckv_g, w_uq, w_ukv)
    ys = jnp.stack([o_a, o_b, o_c], axis=2)
    y = jnp.einsum("bsnw,nwd->bsnd", ys, w_branch)
    gates = jax.nn.sigmoid(gate_logits.astype(jnp.float32)).reshape(B, S, N_BRANCH, D)
    merged = jnp.sum(gates.astype(y.dtype) * y, axis=2)
    return merged @ w_out


def setup_inputs(seed: int = 0) -> dict:
    key = jax.random.key(seed)
    ks = jax.random.split(key, 24)
    f32 = jnp.float32

    def nrm(k, shape, fan_in):
        return jax.random.normal(k, shape, f32) * (fan_in ** -0.5)

    def gain(k, shape):
        return 1.0 + 0.05 * jax.random.normal(k, shape, f32)

    x = jax.random.normal(ks[0], (BATCH, SEQ, D_MODEL), f32)
    offset = jax.random.randint(ks[1], (BATCH, 1), 0, 64, dtype=jnp.int32) * CHUNK
    positions = (offset + jnp.arange(SEQ, dtype=jnp.int32)[None, :]).astype(jnp.int32)
    dt = jnp.exp(jax.random.uniform(ks[11], (DEPTH, DN_HEADS), f32, math.log(1e-3), math.log(1e-1)))
    return {
        "x": x,
        "positions": positions,
        "norm_g": gain(ks[2], (DEPTH, 6, D_MODEL)),
        "ffn_w_gate": nrm(ks[3], (DEPTH, 2, D_MODEL, D_FF), D_MODEL),
        "ffn_w_up": nrm(ks[4], (DEPTH, 2, D_MODEL, D_FF), D_MODEL),
        "ffn_w_down": nrm(ks[5], (DEPTH, 2, D_FF, D_MODEL), D_FF),
        "w_in": nrm(ks[6], (DEPTH, D_MODEL, IN_COLS), D_MODEL),
        "dn_conv_w": nrm(ks[7], (DEPTH, DN_CONV, 3 * DN_WIDTH), DN_CONV),
        "dn_a_log": jnp.log(jax.random.uniform(ks[8], (DEPTH, DN_HEADS), f32, 1.0, 16.0)),
        "dn_dt_bias": dt + jnp.log(-jnp.expm1(-dt)),
        "dn_norm_g": gain(ks[9], (DEPTH, DN_HEAD_DIM)),
        "sg_ln_g": gain(ks[10], (DEPTH, SG_WIDTH)),
        "sg_ln_b": 0.02 * jax.random.normal(ks[12], (DEPTH, SG_WIDTH), f32),
        "sg_w": nrm(ks[13], (DEPTH, SG_GROUPS, SG_BLOCK, SG_BLOCK), SG_BLOCK),
        "sg_b": 1.0 + 0.1 * jax.random.normal(ks[14], (DEPTH, SG_GROUPS, SG_BLOCK), f32),
        "mla_cq_norm_g": gain(ks[15], (DEPTH, MLA_Q_RANK)),
        "mla_ckv_norm_g": gain(ks[16], (DEPTH, MLA_KV_RANK)),
        "mla_w_uq": nrm(ks[17], (DEPTH, MLA_Q_RANK, MLA_HEADS * (MLA_NOPE + MLA_ROPE)), MLA_Q_RANK),
        "mla_w_ukv": nrm(ks[18], (DEPTH, MLA_KV_RANK, MLA_HEADS * (MLA_NOPE + MLA_V)), MLA_KV_RANK),
        "w_branch": nrm(ks[19], (DEPTH, N_BRANCH, BRANCH_WIDTH, D_MODEL), BRANCH_WIDTH),
        "w_out": nrm(ks[20], (DEPTH, D_MODEL, D_MODEL), D_MODEL),
    }


def reference(x, positions, norm_g, ffn_w_gate, ffn_w_up, ffn_w_down, w_in, dn_conv_w, dn_a_log,
              dn_dt_bias, dn_norm_g, sg_ln_g, sg_ln_b, sg_w, sg_b, mla_cq_norm_g, mla_ckv_norm_g,
              mla_w_uq, mla_w_ukv, w_branch, w_out):
    inv_freq = jnp.power(ROPE_THETA, -jnp.arange(0, MLA_ROPE, 2, dtype=jnp.float32) / MLA_ROPE)
    ang = positions.astype(jnp.float32)[..., None] * inv_freq
    cos, sin = jnp.cos(ang), jnp.sin(ang)
    for l in range(DEPTH):
        ng = norm_g[l]
        f = swiglu(rms_norm(x, ng[0]), ffn_w_gate[l, 0], ffn_w_up[l, 0], ffn_w_down[l, 0])
        x = x + 0.5 * rms_norm(f, ng[1])
        h = rms_norm(x, ng[2])
        m = hybrid_mixer(h, cos, sin, w_in[l], dn_conv_w[l], dn_a_log[l], dn_dt_bias[l], dn_norm_g[l],
                         sg_ln_g[l], sg_ln_b[l], sg_w[l], sg_b[l], mla_cq_norm_g[l], mla_ckv_norm_g[l],
                         mla_w_uq[l], mla_w_ukv[l], w_branch[l], w_out[l])
        x = x + rms_norm(m, ng[3])
        f = swiglu(rms_norm(x, ng[4]), ffn_w_gate[l, 1], ffn_w_up[l, 1], ffn_w_down[l, 1])
        x = x + 0.5 * rms_norm(f, ng[5])
    return x
```

```python
from contextlib import ExitStack

import numpy as np
import concourse.bass as bass
import concourse.mybir as mybir
from concourse.bass_utils import run_bass_kernel_spmd

F32 = mybir.dt.float32
BF16 = mybir.dt.bfloat16
I32 = mybir.dt.int32
AF = mybir.ActivationFunctionType
ALU = mybir.AluOpType
AX = mybir.AxisListType

D = 2048
DC = 16
DFF = 5632
NFT = 44
DEPTH = 2
NH = 8
IN_COLS = 13392
O_QKV = 0
O_Z = 3072
O_A = 4096
O_B = 4104
O_SG = 4112
O_CQ = 6160
O_CKV = 6672
O_KR = 7184
O_GATE = 7248
EPS = 1e-6

ENGS = ("pe", "act", "dve", "pool", "sp")


class Buf:
    __slots__ = ("name", "last_write", "reads", "excl")

    def __init__(self, name="", excl=False):
        self.name = name
        self.last_write = None
        self.reads = []
        self.excl = excl


class Op:
    __slots__ = ("eng", "fn", "deps", "is_dma", "idx", "sem_key", "sem_val", "signal", "clock")

    def __init__(self, eng, fn, is_dma):
        self.eng = eng
        self.fn = fn
        self.is_dma = is_dma
        self.deps = []
        self.signal = True
        self.clock = None


class Prog:
    def __init__(self, nc, n_dma_sems=16):
        self.nc = nc
        self.ops = []
        self.n_dma_sems = n_dma_sems
        self.stack = ExitStack()
        self.G = Buf("G")

    def op(self, eng, fn, reads=(), writes=(), is_dma=False, signal=True, barrier=False):
        o = Op(eng, fn, is_dma)
        o.signal = signal
        writes = list(writes) + [b for b in reads if b.excl]
        reads = [b for b in reads if not b.excl]
        if barrier:
            writes.append(self.G)
        else:
            reads.append(self.G)
        deps = []
        for b in reads:
            if b.last_write is not None:
                deps.append(b.last_write)
        for b in writes:
            if b.last_write is not None:
                deps.append(b.last_write)
            deps.extend(b.reads)
        for b in reads:
            b.reads.append(o)
        for b in writes:
            b.last_write = o
            b.reads = []
        seen = set()
        for d in deps:
            if id(d) not in seen and d is not o:
                seen.add(id(d))
                o.deps.append(d)
        o.idx = len(self.ops)
        self.ops.append(o)
        return o

    def dma(self, queue, out, in_, reads=(), writes=(), **kw):
        return self.op(queue, lambda e: e.dma_start(out=out, in_=in_, **kw), reads, writes, is_dma=True)

    def emit(self):
        nc = self.nc
        st = self.stack
        per_eng = {e: [] for e in ENGS}
        for o in self.ops:
            per_eng[o.eng].append(o)
        esem = {e: st.enter_context(nc.semaphore(f"s_{e}")) for e in ENGS}
        dsem = {e: [st.enter_context(nc.semaphore(f"d_{e}{i}")) for i in range(self.n_dma_sems)]
                for e in ("sp", "act", "pool")}
        nxt_sig = {}
        for e in ENGS:
            nxt = None
            for o in reversed(per_eng[e]):
                if o.is_dma or o.signal:
                    nxt = o
                nxt_sig[id(o)] = nxt
        for o in self.ops:
            nd = []
            for d in o.deps:
                if not d.is_dma and not d.signal:
                    r = nxt_sig[id(d)]
                    if r is None or r.idx >= o.idx or r.is_dma:
                        d.signal = True
                        nd.append(d)
                    else:
                        nd.append(r)
                else:
                    nd.append(d)
            o.deps = nd
        cnt = {e: 0 for e in ENGS}
        dcnt = {e: 0 for e in ("sp", "act", "pool")}
        dval = {}
        dlast = {}
        for o in self.ops:
            if o.is_dma:
                i = dcnt[o.eng] % self.n_dma_sems
                dcnt[o.eng] += 1
                key = ("d", o.eng, i)
                prev = dlast.get(key)
                if prev is not None:
                    o.deps.append(prev)
                dlast[key] = o
                dval[key] = dval.get(key, 0) + 16
                o.sem_key = key
                o.sem_val = dval[key]
            elif o.signal:
                cnt[o.eng] += 1
                o.sem_key = ("e", o.eng)
                o.sem_val = cnt[o.eng]
            else:
                o.sem_key = None
                o.sem_val = 0
        known = {e: {} for e in ENGS}
        waits = {}
        for o in self.ops:
            k = known[o.eng]
            w = {}
            for d in o.deps:
                if d.eng == "pe" and o.eng == "pe" and not d.is_dma:
                    continue
                if k.get(d.sem_key, 0) >= d.sem_val:
                    continue
                if w.get(d.sem_key, 0) < d.sem_val:
                    w[d.sem_key] = d.sem_val
            for d in o.deps:
                if d.sem_key in w and d.clock:
                    for kk, vv in d.clock.items():
                        if k.get(kk, 0) < vv:
                            k[kk] = vv
            for kk, vv in w.items():
                if k.get(kk, 0) < vv:
                    k[kk] = vv
            waits[id(o)] = w
            clk = dict(k)
            if o.sem_key is not None:
                clk[o.sem_key] = o.sem_val
            o.clock = clk

        def sem_of(key):
            if key[0] == "e":
                return esem[key[1]]
            return dsem[key[1]][key[2]]

        def run(e, eng):
            for o in per_eng[e]:
                for kk, vv in waits[id(o)].items():
                    eng.wait_ge(sem_of(kk), vv)
                ins = o.fn(eng)
                if o.is_dma:
                    ins.then_inc(sem_of(o.sem_key), 16)
                elif o.signal:
                    ins.then_inc(sem_of(o.sem_key), 1)

        block = st.enter_context(nc.Block())

        @block.tensor
        def _(eng):
            run("pe", eng)

        @block.scalar
        def _(eng):
            run("act", eng)

        @block.vector
        def _(eng):
            run("dve", eng)

        @block.gpsimd
        def _(eng):
            run("pool", eng)

        @block.sync
        def _(eng):
            run("sp", eng)
            for key, v in dval.items():
                eng.wait_ge(sem_of(key), v)

        st.close()
        self.stats = {e: len(per_eng[e]) for e in ENGS}
        return nc


class T:
    __slots__ = ("ap", "buf")

    def __init__(self, ap, buf=None):
        self.ap = ap
        self.buf = buf or Buf()


ARENA_W = 49 * 1024


class KB:
    def __init__(self, S, depth=DEPTH, stages="all", debug=False):
        self.S = S
        self.debug = debug
        self.depth = depth
        self.stages = stages
        nc = self.nc = bass.Bass("TRN2", target_bir_lowering=False)
        P = self.P = Prog(nc)
        st = P.stack
        self.arena = st.enter_context(nc.sbuf_tensor("arena", [128, ARENA_W], F32))
        self.psum = st.enter_context(nc.psum_tensor("psum", [128, 4096], F32))
        self.pb = [Buf(f"ps{i}", excl=True) for i in range(8)]
        self.off = 0
        self.dbufs = {}

        def dram_in(name, shape, dt=F32):
            return nc.dram_tensor(name, list(shape), dt, kind="ExternalInput").ap()

        self.x_in = dram_in("x", [S, D])
        self.pos_in = dram_in("positions", [1, S], I32)
        self.norm_g = dram_in("norm_g", [DEPTH, 6, D])
        self.w_gate = dram_in("ffn_w_gate", [DEPTH, 2, D, DFF])
        self.w_up = dram_in("ffn_w_up", [DEPTH, 2, D, DFF])
        self.w_down = dram_in("ffn_w_down", [DEPTH, 2, DFF, D])
        self.w_in = dram_in("w_in", [DEPTH, D, IN_COLS])
        self.conv_w = dram_in("dn_conv_w", [DEPTH, 4, 3072])
        self.a_log = dram_in("dn_a_log", [DEPTH, 8])
        self.dt_bias = dram_in("dn_dt_bias", [DEPTH, 8])
        self.dn_g = dram_in("dn_norm_g", [DEPTH, 128])
        self.ln_g = dram_in("sg_ln_g", [DEPTH, 1024])
        self.ln_b = dram_in("sg_ln_b", [DEPTH, 1024])
        self.sg_w = dram_in("sg_w", [DEPTH, 8, 128, 128])
        self.sg_b = dram_in("sg_b", [DEPTH, 8, 128])
        self.cq_g = dram_in("mla_cq_norm_g", [DEPTH, 512])
        self.ckv_g = dram_in("mla_ckv_norm_g", [DEPTH, 512])
        self.w_uq = dram_in("mla_w_uq", [DEPTH, 512, 1536])
        self.w_ukv = dram_in("mla_w_ukv", [DEPTH, 512, 2048])
        self.w_branch = dram_in("w_branch", [DEPTH, 3, 1024, D])
        self.w_out = dram_in("w_out", [DEPTH, D, D])
        self.cst = dram_in("consts", [128, 1024])
        self.out = nc.dram_tensor("out", [S, D], F32, kind="ExternalOutput").ap()
        self.xT = nc.dram_tensor("xT", [DC, 128, S], F32).ap()
        self.fT = nc.dram_tensor("fT", [DC, 128, S], F32).ap()
        self.obT = (nc.dram_tensor("obT", [3, 8, 128, S], BF16, kind="ExternalOutput") if debug else
                    nc.dram_tensor("obT", [3, 8, 128, S], BF16)).ap()

        self.ident32 = self.alloc([128], F32)
        self.ones32 = self.alloc([128], F32)
        self.ident16 = self.alloc([128], BF16)
        self.epsc = self.alloc([1], F32)
        self.gcols = self.alloc([DEPTH * 6, DC], F32)
        self.persist = self.off
        self.setup_consts()

    def alloc(self, shape, dt, name=""):
        n = 1
        for s in shape:
            n *= s
        nbytes = n * (4 if dt in (F32, I32) else 2)
        words = (nbytes + 3) // 4
        words = (words + 7) // 8 * 8
        assert self.off + words <= ARENA_W, f"SBUF arena overflow {self.off}+{words}"
        ap = self.arena[:, self.off:self.off + words]
        self.off += words
        if dt != F32:
            ap = ap.bitcast(dt)
        ap = ap[:, 0:n]
        if len(shape) == 2:
            ap = ap.rearrange("p (a b) -> p a b", b=shape[1])
        elif len(shape) == 3:
            ap = ap.rearrange("p (a b c) -> p a b c", b=shape[1], c=shape[2])
        return T(ap)

    def ring(self, n, shape, dt):
        return [self.alloc(shape, dt) for _ in range(n)]

    def bank(self, i, n=1):
        return self.psum[:, 512 * i:512 * (i + n)], self.pb[i:i + n]

    def db(self, *key):
        b = self.dbufs.get(key)
        if b is None:
            b = self.dbufs[key] = Buf(str(key))
        return b

    def barrier(self):
        t = self.epsc
        self.P.op("pool", lambda e: e.memset(t.ap[:, 0:1], EPS), [], [t.buf], barrier=True)

    def stage_reset(self):
        self.barrier()
        self.off = self.persist

    def setup_consts(self):
        P = self.P
        c = self.cst
        P.dma("sp", self.ident32.ap, c[:, 0:128], writes=[self.ident32.buf])
        P.dma("sp", self.ones32.ap, c[:, 128:256], writes=[self.ones32.buf])
        P.dma("pool", self.ident16.ap, c[:, 0:128], writes=[self.ident16.buf])
        self.barrier()
        rows = self.alloc([D], F32)
        R = DEPTH * 6
        P.dma("sp", rows.ap[0:R, :], self.norm_g.rearrange("l k d -> (l k) d"), writes=[rows.buf])
        pa, pbuf = self.bank(0)
        for dc in range(DC):
            P.op("pe", lambda e, dc=dc: e.transpose(pa[:, dc * R:(dc + 1) * R], rows.ap[0:R, dc * 128:(dc + 1) * 128],
                                                   self.ident32.ap[0:R, 0:R]),
                 [rows.buf, self.ident32.buf], pbuf)
        g = self.gcols
        P.op("dve", lambda e: e.tensor_copy(out=g.ap.rearrange("p r c -> p c r"),
                                            in_=pa[:, 0:DC * R].rearrange("p (c r) -> p c r", r=R)),
             pbuf, [g.buf])
        self.off = self.persist

    def gcol(self, l, k, dc):
        r = l * 6 + k
        return self.gcols.ap[:, r, dc:dc + 1]

    def load_x(self):
        P = self.P
        S = self.S
        self.stage_reset()
        xin = self.ring(2, [D], F32)
        stg = self.ring(3, [4, 128], F32)
        k = 0
        for tt in range(S // 128):
            xt = xin[tt % 2]
            P.dma("sp", xt.ap, self.x_in[tt * 128:(tt + 1) * 128, :], writes=[xt.buf])
            for q in range(4):
                pa, pbuf = self.bank(k % 8)
                for j in range(4):
                    dc = 4 * q + j
                    P.op("pe", lambda e, pa=pa, j=j, dc=dc, xt=xt: e.transpose(
                        pa[:, j * 128:(j + 1) * 128], xt.ap[:, dc * 128:(dc + 1) * 128], self.ident32.ap),
                        [xt.buf, self.ident32.buf], pbuf, signal=(j == 3))
                sg = stg[k % 3]
                eng = "act" if k % 2 == 0 else "dve"
                if eng == "act":
                    P.op("act", lambda e, sg=sg, pa=pa: e.activation(out=sg.ap.rearrange("p a b -> p (a b)"), in_=pa, func=AF.Copy),
                         pbuf, [sg.buf])
                else:
                    P.op("dve", lambda e, sg=sg, pa=pa: e.tensor_copy(out=sg.ap.rearrange("p a b -> p (a b)"), in_=pa),
                         pbuf, [sg.buf])
                P.dma("sp", self.xT[4 * q:4 * q + 4, :, tt * 128:(tt + 1) * 128].rearrange("c p t -> p c t"), sg.ap,
                      reads=[sg.buf], writes=[self.db("xT", dc_, tt // 4) for dc_ in range(4 * q, 4 * q + 4)])
                k += 1

    def store_out(self):
        P = self.P
        S = self.S
        self.stage_reset()
        xs = self.ring(2, [DC, 512], F32)
        ot = self.ring(2, [D], F32)
        k = 0
        for tb in range(S // 512):
            x = xs[tb % 2]
            P.dma("sp", x.ap, self.xT[:, :, tb * 512:(tb + 1) * 512].rearrange("c p t -> p c t"),
                  reads=[self.db("xT", dc, tb) for dc in range(DC)], writes=[x.buf])
            for t4 in range(4):
                o = ot[(tb * 4 + t4) % 2]
                for q in range(4):
                    pa, pbuf = self.bank(k % 8)
                    for j in range(4):
                        dc = 4 * q + j
                        P.op("pe", lambda e, pa=pa, j=j, dc=dc, x=x, t4=t4: e.transpose(
                            pa[:, j * 128:(j + 1) * 128], x.ap[:, dc, t4 * 128:(t4 + 1) * 128], self.ident32.ap),
                            [x.buf, self.ident32.buf], pbuf, signal=(j == 3))
                    if k % 2 == 0:
                        P.op("act", lambda e, o=o, pa=pa, q=q: e.activation(out=o.ap[:, q * 512:(q + 1) * 512], in_=pa, func=AF.Copy),
                             pbuf, [o.buf])
                    else:
                        P.op("dve", lambda e, o=o, pa=pa, q=q: e.tensor_copy(out=o.ap[:, q * 512:(q + 1) * 512], in_=pa),
                             pbuf, [o.buf])
                    k += 1
                r0 = tb * 512 + t4 * 128
                P.dma("sp", self.out[r0:r0 + 128, :], o.ap, reads=[o.buf], writes=[self.db("out", r0)])

    def rstd_from(self, pa, pbufs, rstd, n):
        P = self.P
        P.op("act", lambda e: e.activation(out=rstd.ap, in_=pa, func=AF.Sqrt, bias=self.epsc.ap[:, 0:1], scale=1.0 / n),
             list(pbufs) + [self.epsc.buf], [rstd.buf])
        P.op("dve", lambda e: e.reciprocal(out=rstd.ap, in_=rstd.ap), [rstd.buf], [rstd.buf])

    def residual(self, l, k, t0, TS, rstd, tmp):
        P = self.P
        xr = tmp[0:3]
        fr = tmp[3:6]
        nb = TS // 512
        for dc in range(DC):
            x = xr[dc % 3]
            f = fr[dc % 3]
            xb = [self.db("xT", dc, t0 // 512 + i) for i in range(nb)]
            fb = [self.db("fT", dc, t0 // 512 + i) for i in range(nb)]
            P.dma("sp", x.ap, self.xT[dc, :, t0:t0 + TS], reads=xb, writes=[x.buf])
            P.dma("sp", f.ap, self.fT[dc, :, t0:t0 + TS], reads=fb, writes=[f.buf])
            P.op("dve", lambda e, f=f, dc=dc: e.scalar_tensor_tensor(out=f.ap, in0=f.ap, scalar=self.gcol(l, k, dc), in1=rstd.ap,
                                                                    op0=ALU.mult, op1=ALU.mult),
                 [f.buf, rstd.buf, self.gcols.buf], [f.buf])
            P.op("pool", lambda e, f=f, x=x: e.tensor_tensor(out=x.ap, in0=f.ap, in1=x.ap, op=ALU.add),
                 [f.buf, x.buf], [x.buf])
            P.dma("sp", self.xT[dc, :, t0:t0 + TS], x.ap, reads=[x.buf], writes=xb)

    def prenorm(self, l, k, t0, TS, hT, tmp):
        P = self.P
        nb = TS // 512
        xr = tmp[0:3]
        sq = tmp[3:5]
        rstd = tmp[5]
        pa, pbufs = self.bank(0, nb)
        for dc in range(DC):
            x = xr[dc % 3]
            s = sq[dc % 2]
            xb = [self.db("xT", dc, t0 // 512 + i) for i in range(nb)]
            P.dma("sp", x.ap, self.xT[dc, :, t0:t0 + TS], reads=xb, writes=[x.buf])
            P.op("act", lambda e, x=x, s=s: e.activation(out=s.ap, in_=x.ap, func=AF.Square), [x.buf], [s.buf])
            for i in range(nb):
                P.op("pe", lambda e, s=s, i=i, dc=dc: e.matmul(pa[:, i * 512:(i + 1) * 512], self.ones32.ap,
                                                              s.ap[:, i * 512:(i + 1) * 512], start=(dc == 0), stop=(dc == DC - 1)),
                     [s.buf, self.ones32.buf], [pbufs[i]], signal=(dc == DC - 1))
        self.rstd_from(pa, pbufs, rstd, D)
        for dc in range(DC):
            x = xr[(dc + 1) % 3]
            xb = [self.db("xT", dc, t0 // 512 + i) for i in range(nb)]
            P.dma("sp", x.ap, self.xT[dc, :, t0:t0 + TS], reads=xb, writes=[x.buf])
            P.op("dve", lambda e, x=x, dc=dc: e.scalar_tensor_tensor(out=hT.ap[:, dc, :], in0=x.ap, scalar=self.gcol(l, k, dc),
                                                                    in1=rstd.ap, op0=ALU.mult, op1=ALU.mult),
                 [x.buf, rstd.buf, self.gcols.buf], [hT.buf])

    def ffn(self, l, i):
        P = self.P
        S = self.S
        TS = min(1024, S)
        nb = TS // 512
        wg = self.w_gate[l, i]
        wu = self.w_up[l, i]
        wd = self.w_down[l, i]
        for sb in range(S // TS):
            t0 = sb * TS
            self.stage_reset()
            hT = self.alloc([DC, TS], BF16)
            AT = self.alloc([NFT, TS], BF16)
            wr = self.ring(3, [NFT * 128], BF16)
            tmp = self.ring(7, [TS], F32)
            self.prenorm(l, 0 if i == 0 else 4, t0, TS, hT, tmp)
            sgr = tmp[0:2]
            for ft in range(NFT):
                w = wr[ft % 3]
                wv = w.ap[:, 0:2 * DC * 128].rearrange("p (g c k) -> p g c k", g=2, c=DC)
                P.dma("pool", wv[:, 0], wg[:, ft * 128:(ft + 1) * 128].rearrange("(c p) k -> p c k", p=128), writes=[w.buf])
                P.dma("pool", wv[:, 1], wu[:, ft * 128:(ft + 1) * 128].rearrange("(c p) k -> p c k", p=128), writes=[w.buf])
                b0 = (ft % 2) * 4
                ga, gb = self.bank(b0, nb)
                ua, ub = self.bank(b0 + 2, nb)
                for (pa, pbs, g) in ((ga, gb, 0), (ua, ub, 1)):
                    for n in range(nb):
                        for dc in range(DC):
                            P.op("pe", lambda e, pa=pa, n=n, dc=dc, wv=wv, g=g: e.matmul(
                                pa[:, n * 512:(n + 1) * 512], wv[:, g, dc, :], hT.ap[:, dc, n * 512:(n + 1) * 512],
                                start=(dc == 0), stop=(dc == DC - 1)),
                                [w.buf, hT.buf], [pbs[n]], signal=(dc == DC - 1))
                sg = sgr[ft % 2]
                P.op("act", lambda e, sg=sg, ga=ga: e.activation(out=sg.ap, in_=ga, func=AF.Silu), gb, [sg.buf])
                P.op("dve", lambda e, sg=sg, ua=ua, ft=ft: e.tensor_tensor(out=AT.ap[:, ft, :], in0=sg.ap, in1=ua, op=ALU.mult),
                     [sg.buf] + list(ub), [AT.buf])
            fr = tmp[2:5]
            sq = tmp[5:7]
            sa, sbufs = self.bank(6, nb)
            for dt in range(DC):
                w = wr[(NFT + dt) % 3]
                wv = w.ap[:, 0:NFT * 128].rearrange("p (f k) -> p f k", k=128)
                P.dma("pool", wv, wd[:, dt * 128:(dt + 1) * 128].rearrange("(f p) k -> p f k", p=128), writes=[w.buf])
                fa, fb = self.bank((dt % 3) * 2, nb)
                for n in range(nb):
                    for ft in range(NFT):
                        P.op("pe", lambda e, fa=fa, n=n, ft=ft, wv=wv: e.matmul(
                            fa[:, n * 512:(n + 1) * 512], wv[:, ft, :], AT.ap[:, ft, n * 512:(n + 1) * 512],
                            start=(ft == 0), stop=(ft == NFT - 1)),
                            [w.buf, AT.buf], [fb[n]], signal=(ft == NFT - 1))
                f = fr[dt % 3]
                s = sq[dt % 2]
                P.op("act", lambda e, f=f, fa=fa: e.activation(out=f.ap, in_=fa, func=AF.Copy), fb, [f.buf])
                P.op("act", lambda e, f=f, s=s: e.activation(out=s.ap, in_=f.ap, func=AF.Square), [f.buf], [s.buf])
                for n in range(nb):
                    P.op("pe", lambda e, s=s, n=n, dt=dt: e.matmul(sa[:, n * 512:(n + 1) * 512], self.ones32.ap,
                                                                  s.ap[:, n * 512:(n + 1) * 512], start=(dt == 0), stop=(dt == DC - 1)),
                         [s.buf, self.ones32.buf], [sbufs[n]], signal=(dt == DC - 1))
                P.dma("sp", self.fT[dt, :, t0:t0 + TS], f.ap, reads=[f.buf],
                      writes=[self.db("fT", dt, t0 // 512 + j) for j in range(nb)])
            rstd = tmp[6]
            self.rstd_from(sa, sbufs, rstd, D)
            P.op("dve", lambda e: e.tensor_scalar(out=rstd.ap, in0=rstd.ap, scalar1=0.5, scalar2=None, op0=ALU.mult),
                 [rstd.buf], [rstd.buf])
            self.residual(l, 1 if i == 0 else 5, t0, TS, rstd, tmp)

    def build(self):
        self.load_x()
        if self.stages in ("all", "mix", "mla"):
            self.rope_tables()
        for l in range(self.depth):
            if "ffn" in self.stages or self.stages == "all":
                self.ffn(l, 0)
            if self.stages in ("all", "mix", "sg", "mla", "dn"):
                self.mixer(l)
            if "ffn" in self.stages or self.stages == "all":
                self.ffn(l, 1)
        self.store_out()
        self.P.emit()
        return self.nc

    def sub_reset(self):
        self.barrier()
        self.off = self.mark

    def wtile(self, dst_ap, src_rows_ap, buf, reads=()):
        self.P.dma("pool", dst_ap, src_rows_ap.rearrange("(c p) k -> p c k", p=128), reads=list(reads), writes=[buf])

    def proj_fm(self, hT, wt, ncol, S, consume, banks=(0, 1, 2, 3), K=DC):
        P = self.P
        for tb in range(S // 512):
            pa, pbs = self.bank(banks[tb % len(banks)])
            pa = pa[0:ncol, :]
            for dc in range(K):
                P.op("pe", lambda e, pa=pa, dc=dc, tb=tb: e.matmul(pa, wt.ap[:, dc, 0:ncol], hT.ap[:, dc, tb * 512:(tb + 1) * 512],
                                                               start=(dc == 0), stop=(dc == K - 1)),
                     [wt.buf, hT.buf], pbs, signal=(dc == K - 1))
            consume(tb, pa, pbs)

    def mixer(self, l):
        S = self.S
        self.stage_reset()
        hT = self.alloc([DC, S], BF16)
        self.mark = self.off
        tmp = self.ring(6, [S], F32)
        self.prenorm(l, 2, 0, S, hT, tmp)
        self.sub_reset()
        if self.stages in ("all", "mix", "sg"):
            self.sg_stage(l, hT)
            self.sub_reset()
        if self.stages in ("all", "mix", "mla"):
            self.mla_proj(l, hT)
            self.sub_reset()
        if self.stages in ("all", "mix", "dn"):
            self.dn_proj(l, hT)
        if self.stages in ("all", "mix", "mla"):
            self.mla_attn(l)
        if self.stages in ("all", "mix", "dn"):
            self.dn_core(l)
        if self.stages in ("all", "mix"):
            self.merge(l)

    def sg_stage(self, l, hT):
        P = self.P
        S = self.S
        W = self.w_in[l]
        wu = self.alloc([DC, 1024], BF16)
        wv = self.alloc([DC, 1024], BF16)
        self.wtile(wu.ap, W[:, O_SG:O_SG + 1024], wu.buf)
        self.wtile(wv.ap, W[:, O_SG + 1024:O_SG + 2048], wv.buf)
        lng = self.alloc([1024], F32)
        lnb = self.alloc([1024], F32)
        bias = self.alloc([8, 128], F32)
        P.dma("sp", lng.ap, self.ln_g[l:l + 1, :].broadcast_to([128, 1024]), writes=[lng.buf])
        P.dma("sp", lnb.ap, self.ln_b[l:l + 1, :].broadcast_to([128, 1024]), writes=[lnb.buf])
        P.dma("sp", bias.ap.rearrange("p g i -> p (g i)"),
              self.sg_b[l:l + 1].rearrange("o g i -> o (g i)").broadcast_to([128, 1024]), writes=[bias.buf])
        wraw = self.alloc([8, 128], F32)
        P.dma("sp", wraw.ap, self.sg_w[l].rearrange("g i j -> i g j"), writes=[wraw.buf])
        smask = self.alloc([128], F32)
        P.dma("sp", smask.ap, self.cst[:, 770:898], writes=[smask.buf])
        eps5 = self.alloc([1], F32)
        P.op("pool", lambda e: e.memset(eps5.ap, 1e-5), [], [eps5.buf])
        wmT = self.alloc([8, 128], BF16)
        for g in range(8):
            pa, pbs = self.bank(g // 4)
            P.op("pe", lambda e, pa=pa, g=g: e.transpose(pa[:, (g % 4) * 128:(g % 4 + 1) * 128], wraw.ap[:, g, :], self.ident32.ap),
                 [wraw.buf, self.ident32.buf], pbs)
            P.op("dve", lambda e, pa=pa, g=g: e.tensor_tensor(out=wmT.ap[:, g, :], in0=pa[:, (g % 4) * 128:(g % 4 + 1) * 128],
                                                             in1=smask.ap, op=ALU.mult),
                 list(pbs) + [smask.buf], [wmT.buf])
        uTr = self.ring(1, [8, 512], F32)
        obr = self.ring(1, [8, 512], BF16)
        vgr = self.ring(2, [1024], F32)
        vlr = self.ring(2, [1024], BF16)
        tmr = self.ring(1, [8, 128], F32)
        st = self.alloc([12], F32)
        mv = self.alloc([2], F32)
        rs = self.alloc([1], F32)
        kk = 0
        for tb in range(S // 512):
            uT = uTr[0]
            ob = obr[0]
            for g in range(8):
                pa, pbs = self.bank(g % 2)
                for dc in range(DC):
                    P.op("pe", lambda e, pa=pa, dc=dc, g=g, tb=tb: e.matmul(pa, wu.ap[:, dc, g * 128:(g + 1) * 128],
                                                                        hT.ap[:, dc, tb * 512:(tb + 1) * 512],
                                                                        start=(dc == 0), stop=(dc == DC - 1)),
                         [wu.buf, hT.buf], pbs, signal=(dc == DC - 1))
                P.op("act", lambda e, pa=pa, g=g, uT=uT: e.activation(out=uT.ap[:, g, :], in_=pa, func=AF.Gelu), pbs, [uT.buf])
            for t4 in range(4):
                tt = tb * 4 + t4
                b0 = 2 + 2 * (kk % 2)
                va, vbs = self.bank(b0, 2)
                for half in range(2):
                    for dc in range(DC):
                        P.op("pe", lambda e, va=va, dc=dc, half=half, tt=tt: e.matmul(
                            va[:, half * 512:(half + 1) * 512], hT.ap[:, dc, tt * 128:(tt + 1) * 128],
                            wv.ap[:, dc, half * 512:(half + 1) * 512], start=(dc == 0), stop=(dc == DC - 1)),
                            [wv.buf, hT.buf], [vbs[half]], signal=(dc == DC - 1))
                vg = vgr[kk % 2]
                vl = vlr[kk % 2]
                tm = tmr[0]
                P.op("act", lambda e, vg=vg, va=va: e.activation(out=vg.ap, in_=va, func=AF.Gelu), vbs, [vg.buf])
                P.op("dve", lambda e, vg=vg: e.bn_stats(out=st.ap[:, 0:6], in_=vg.ap[:, 0:512]), [vg.buf], [st.buf])
                P.op("dve", lambda e, vg=vg: e.bn_stats(out=st.ap[:, 6:12], in_=vg.ap[:, 512:1024]), [vg.buf], [st.buf])
                P.op("dve", lambda e: e.bn_aggr(out=mv.ap, in_=st.ap), [st.buf], [mv.buf])
                P.op("act", lambda e: e.activation(out=rs.ap, in_=mv.ap[:, 1:2], func=AF.Sqrt, bias=eps5.ap[:, 0:1], scale=1.0),
                     [mv.buf, eps5.buf], [rs.buf])
                P.op("dve", lambda e: e.reciprocal(out=rs.ap, in_=rs.ap), [rs.buf], [rs.buf])
                P.op("dve", lambda e, vg=vg: e.tensor_scalar(out=vg.ap, in0=vg.ap, scalar1=mv.ap[:, 0:1], scalar2=rs.ap[:, 0:1],
                                                            op0=ALU.subtract, op1=ALU.mult), [vg.buf, mv.buf, rs.buf], [vg.buf])
                P.op("pool", lambda e, vg=vg: e.tensor_tensor(out=vg.ap, in0=vg.ap, in1=lng.ap, op=ALU.mult), [vg.buf, lng.buf], [vg.buf])
                P.op("pool", lambda e, vg=vg, vl=vl: e.tensor_tensor(out=vl.ap, in0=vg.ap, in1=lnb.ap, op=ALU.add),
                     [vg.buf, lnb.buf], [vl.buf])
                ma, mbs = self.bank(6, 2)
                for g in range(8):
                    P.op("pe", lambda e, g=g, vl=vl: e.matmul(ma[:, g * 128:(g + 1) * 128], vl.ap[:, g * 128:(g + 1) * 128], wmT.ap[:, g, :],
                                                            start=True, stop=True),
                         [vl.buf, wmT.buf], [mbs[g // 4]], signal=(g % 4 == 3))
                P.op("dve", lambda e, tm=tm: e.tensor_tensor(out=tm.ap, in0=ma.rearrange("p (g i) -> p g i", i=128), in1=bias.ap, op=ALU.add),
                     list(mbs) + [bias.buf], [tm.buf])
                P.op("pool", lambda e, tm=tm, ob=ob, uT=uT, t4=t4: e.tensor_tensor(
                    out=ob.ap[:, :, t4 * 128:(t4 + 1) * 128], in0=tm.ap, in1=uT.ap[:, :, t4 * 128:(t4 + 1) * 128], op=ALU.mult),
                    [tm.buf, uT.buf], [ob.buf])
                kk += 1
            P.dma("sp", self.obT[1, :, :, tb * 512:(tb + 1) * 512].rearrange("g p t -> p g t"), ob.ap, reads=[ob.buf],
                  writes=[self.db("obT", 1, tb)])

    def merge(self, l):
        P = self.P
        S = self.S
        TS = min(1024, S)
        nb = TS // 512
        W = self.w_in[l]
        for sb in range(S // TS):
            t0 = sb * TS
            self.stage_reset()
            hT = self.alloc([DC, TS], BF16)
            mT = self.alloc([DC, TS], BF16)
            ob = self.alloc([24, TS], BF16)
            tmp = self.ring(7, [TS], F32)
            wr = self.ring(4, [DC * 128], BF16)
            self.prenorm(l, 2, t0, TS, hT, tmp)
            P.dma("sp", ob.ap, self.obT[:, :, :, t0:t0 + TS].rearrange("n g p t -> p (n g) t"),
                  reads=[self.db("obT", n, t0 // 512 + j) for n in range(3) for j in range(nb)], writes=[ob.buf])
            wk = 0
            for dt in range(DC):
                acc = tmp[dt % 2]
                for n in range(3):
                    wg = wr[wk % 4]; wk += 1
                    wgv = wg.ap.rearrange("p (c k) -> p c k", k=128)
                    c0 = O_GATE + n * D + dt * 128
                    self.wtile(wgv, W[:, c0:c0 + 128], wg.buf)
                    wb = wr[wk % 4]; wk += 1
                    wbv = wb.ap[:, 0:8 * 128].rearrange("p (c k) -> p c k", k=128)
                    self.wtile(wbv, self.w_branch[l, n][:, dt * 128:(dt + 1) * 128], wb.buf)
                    ga, gbs = self.bank(0 if n % 2 == 0 else 4, nb)
                    ya, ybs = self.bank(2 if n % 2 == 0 else 6, nb)
                    for j in range(nb):
                        for dc in range(DC):
                            P.op("pe", lambda e, ga=ga, j=j, dc=dc, wgv=wgv: e.matmul(ga[:, j * 512:(j + 1) * 512], wgv[:, dc, :],
                                                                                  hT.ap[:, dc, j * 512:(j + 1) * 512],
                                                                                  start=(dc == 0), stop=(dc == DC - 1)),
                                 [wg.buf, hT.buf], [gbs[j]], signal=(dc == DC - 1))
                    for j in range(nb):
                        for wc in range(8):
                            P.op("pe", lambda e, ya=ya, j=j, wc=wc, wbv=wbv, n=n: e.matmul(ya[:, j * 512:(j + 1) * 512], wbv[:, wc, :],
                                                                                       ob.ap[:, n * 8 + wc, j * 512:(j + 1) * 512],
                                                                                       start=(wc == 0), stop=(wc == 7)),
                                 [wb.buf, ob.buf], [ybs[j]], signal=(wc == 7))
                    gt = tmp[2 + n % 2]
                    P.op("act", lambda e, gt=gt, ga=ga: e.activation(out=gt.ap, in_=ga, func=AF.Sigmoid), gbs, [gt.buf])
                    if n == 0:
                        P.op("dve", lambda e, gt=gt, ya=ya, acc=acc: e.tensor_tensor(out=acc.ap, in0=gt.ap, in1=ya, op=ALU.mult),
                             [gt.buf] + list(ybs), [acc.buf])
                    else:
                        P.op("dve", lambda e, gt=gt, ya=ya: e.tensor_tensor(out=gt.ap, in0=gt.ap, in1=ya, op=ALU.mult),
                             [gt.buf] + list(ybs), [gt.buf])
                        if n == 1:
                            P.op("pool", lambda e, gt=gt, acc=acc: e.tensor_tensor(out=acc.ap, in0=acc.ap, in1=gt.ap, op=ALU.add),
                                 [gt.buf, acc.buf], [acc.buf])
                        else:
                            P.op("pool", lambda e, gt=gt, acc=acc, dt=dt: e.tensor_tensor(out=mT.ap[:, dt, :], in0=acc.ap, in1=gt.ap, op=ALU.add),
                                 [gt.buf, acc.buf], [mT.buf])
            fr = tmp[2:5]
            sq = tmp[5:7]
            sa, sbufs = self.bank(6, nb)
            for dt in range(DC):
                w = wr[wk % 4]; wk += 1
                wv = w.ap.rearrange("p (c k) -> p c k", k=128)
                self.wtile(wv, self.w_out[l][:, dt * 128:(dt + 1) * 128], w.buf)
                fa, fb = self.bank((dt % 3) * 2, nb)
                for j in range(nb):
                    for dc in range(DC):
                        P.op("pe", lambda e, fa=fa, j=j, dc=dc, wv=wv: e.matmul(fa[:, j * 512:(j + 1) * 512], wv[:, dc, :],
                                                                             mT.ap[:, dc, j * 512:(j + 1) * 512],
                                                                             start=(dc == 0), stop=(dc == DC - 1)),
                             [w.buf, mT.buf], [fb[j]], signal=(dc == DC - 1))
                f = fr[dt % 3]
                s = sq[dt % 2]
                P.op("act", lambda e, f=f, fa=fa: e.activation(out=f.ap, in_=fa, func=AF.Copy), fb, [f.buf])
                P.op("act", lambda e, f=f, s=s: e.activation(out=s.ap, in_=f.ap, func=AF.Square), [f.buf], [s.buf])
                for j in range(nb):
                    P.op("pe", lambda e, s=s, j=j, dt=dt: e.matmul(sa[:, j * 512:(j + 1) * 512], self.ones32.ap,
                                                                  s.ap[:, j * 512:(j + 1) * 512], start=(dt == 0), stop=(dt == DC - 1)),
                         [s.buf, self.ones32.buf], [sbufs[j]], signal=(dt == DC - 1))
                P.dma("sp", self.fT[dt, :, t0:t0 + TS], f.ap, reads=[f.buf],
                      writes=[self.db("fT", dt, t0 // 512 + j) for j in range(nb)])
            rstd = tmp[6]
            self.rstd_from(sa, sbufs, rstd, D)
            self.residual(l, 3, t0, TS, rstd, tmp)

    def cols_from_rows(self, src_ap, R, n, bank=7):
        P = self.P
        rows = self.alloc([n * 128], F32)
        P.dma("sp", rows.ap[0:R, :], src_ap, writes=[rows.buf])
        out = self.alloc([n, R], F32)
        for c0 in range(0, n, 512 // R):
            c1 = min(n, c0 + 512 // R)
            pa, pbs = self.bank(bank)
            for c in range(c0, c1):
                P.op("pe", lambda e, pa=pa, c=c, c0=c0: e.transpose(pa[:, (c - c0) * R:(c - c0 + 1) * R], rows.ap[0:R, c * 128:(c + 1) * 128],
                                                                 self.ident32.ap[0:R, 0:R]),
                     [rows.buf, self.ident32.buf], pbs)
            P.op("dve", lambda e, pa=pa, c0=c0, c1=c1: e.tensor_copy(out=out.ap[:, c0:c1, :],
                                                                    in_=pa[:, 0:(c1 - c0) * R].rearrange("p (c r) -> p c r", r=R)),
                 pbs, [out.buf])
        return out

    def rope_tables(self):
        P = self.P
        S = self.S
        self.ropeT = self.nc.dram_tensor("ropeT", [2, 64, S], F32).ap()
        self.stage_reset()
        PI = 3.14159265358979
        C1 = 6.28125
        C2 = 2 * PI - C1
        MAGIC = 12582912.0
        PIC = 3.1415925
        posi = self.alloc([S], I32)
        P.dma("sp", posi.ap[0:64, :], self.pos_in.broadcast_to([64, S]), writes=[posi.buf])
        cc = self.alloc([2], F32)
        P.dma("sp", cc.ap, self.cst[:, 768:770], writes=[cc.buf])
        ang = self.alloc([S], F32)
        k = self.alloc([S], F32)
        r = self.alloc([S], F32)
        rc = self.alloc([S], F32)
        m = self.alloc([S], F32)
        a = lambda t: t.ap[0:64, :]
        V = lambda fn, rd, wr: P.op("dve", fn, [t.buf for t in rd], [t.buf for t in wr])
        V(lambda e: e.tensor_copy(out=a(ang), in_=a(posi)), [posi], [ang])
        V(lambda e: e.tensor_scalar(out=a(ang), in0=a(ang), scalar1=cc.ap[0:64, 0:1], scalar2=None, op0=ALU.mult), [ang, cc], [ang])
        V(lambda e: e.tensor_scalar(out=a(k), in0=a(ang), scalar1=1.0 / (2 * PI), scalar2=MAGIC, op0=ALU.mult, op1=ALU.add), [ang], [k])
        V(lambda e: e.tensor_scalar(out=a(k), in0=a(k), scalar1=-MAGIC, scalar2=None, op0=ALU.add), [k], [k])
        V(lambda e: e.scalar_tensor_tensor(out=a(r), in0=a(k), scalar=-C1, in1=a(ang), op0=ALU.mult, op1=ALU.add), [k, ang], [r])
        V(lambda e: e.scalar_tensor_tensor(out=a(r), in0=a(k), scalar=-C2, in1=a(r), op0=ALU.mult, op1=ALU.add), [k, r], [r])
        V(lambda e: e.tensor_scalar(out=a(rc), in0=a(r), scalar1=PI / 2, scalar2=None, op0=ALU.add), [r], [rc])
        V(lambda e: e.tensor_scalar(out=a(m), in0=a(rc), scalar1=PI, scalar2=-2 * PI, op0=ALU.is_gt, op1=ALU.mult), [rc], [m])
        V(lambda e: e.tensor_tensor(out=a(rc), in0=a(rc), in1=a(m), op=ALU.add), [rc, m], [rc])
        for t in (r, rc):
            V(lambda e, t=t: e.tensor_scalar(out=a(t), in0=a(t), scalar1=-PIC, scalar2=PIC, op0=ALU.max, op1=ALU.min), [t], [t])
        P.op("act", lambda e: e.activation(out=a(r), in_=a(r), func=AF.Sin), [r.buf], [r.buf])
        P.op("act", lambda e: e.activation(out=a(rc), in_=a(rc), func=AF.Sin), [rc.buf], [rc.buf])
        V(lambda e: e.tensor_scalar(out=a(r), in0=a(r), scalar1=cc.ap[0:64, 1:2], scalar2=None, op0=ALU.mult), [r, cc], [r])
        P.dma("sp", self.ropeT[0], a(rc), reads=[rc.buf], writes=[self.db("rope", 0)])
        P.dma("sp", self.ropeT[1], a(r), reads=[r.buf], writes=[self.db("rope", 1)])

    def mla_proj(self, l, hT):
        P = self.P
        S = self.S
        W = self.w_in[l]
        if not hasattr(self, "cqnT"):
            self.cqnT = self.nc.dram_tensor("cqnT", [2, 4, 128, S], BF16).ap()
            self.krT = self.nc.dram_tensor("krT", [64, S], BF16).ap()
        gq = self.cols_from_rows(self.cq_g[l:l + 1, :].broadcast_to([2, 512]), 2, 4)
        gk = self.cols_from_rows(self.ckv_g[l:l + 1, :].broadcast_to([2, 512]), 2, 4)
        wr = self.ring(3, [DC * 128], BF16)
        raw = self.alloc([4, S], F32)
        o16 = self.alloc([4, S], BF16)
        sqr = self.ring(2, [S], F32)
        rstd = self.alloc([S], F32)
        nb = S // 512
        for wi, (off, g) in enumerate(((O_CQ, gq), (O_CKV, gk))):
            for rc in range(4):
                wt = wr[rc % 3]
                wt3 = T(wt.ap.rearrange("p (c k) -> p c k", k=128), wt.buf)
                self.wtile(wt3.ap, W[:, off + rc * 128:off + (rc + 1) * 128], wt.buf)

                def consume(tb, pa, pbs, rc=rc):
                    P.op("act", lambda e: e.activation(out=raw.ap[:, rc, tb * 512:(tb + 1) * 512], in_=pa, func=AF.Copy), pbs, [raw.buf])
                self.proj_fm(hT, wt3, 128, S, consume)
            sa, sbufs = self.bank(4, nb)
            for rc in range(4):
                sq = sqr[rc % 2]
                P.op("act", lambda e, sq=sq, rc=rc: e.activation(out=sq.ap, in_=raw.ap[:, rc, :], func=AF.Square), [raw.buf], [sq.buf])
                for j in range(nb):
                    P.op("pe", lambda e, sq=sq, j=j, rc=rc: e.matmul(sa[:, j * 512:(j + 1) * 512], self.ones32.ap, sq.ap[:, j * 512:(j + 1) * 512],
                                                                    start=(rc == 0), stop=(rc == 3)),
                         [sq.buf, self.ones32.buf], [sbufs[j]], signal=(rc == 3))
            self.rstd_from(sa, sbufs, rstd, 512)
            for rc in range(4):
                P.op("dve", lambda e, rc=rc, g=g: e.scalar_tensor_tensor(out=o16.ap[:, rc, :], in0=raw.ap[:, rc, :], scalar=g.ap[:, rc, 0:1],
                                                                        in1=rstd.ap, op0=ALU.mult, op1=ALU.mult),
                     [raw.buf, g.buf, rstd.buf], [o16.buf])
            P.dma("sp", self.cqnT[wi].rearrange("c p t -> p c t"), o16.ap, reads=[o16.buf], writes=[self.db("cqnT", wi)])
        wa = self.alloc([DC, 64], BF16)
        wb = self.alloc([DC, 64], BF16)
        self.wtile(wa.ap, W[:, O_KR:O_KR + 64], wa.buf)
        self.wtile(wb.ap[:, :, 0:32], W[:, O_KR + 32:O_KR + 64], wb.buf)
        self.wtile(wb.ap[:, :, 32:64], W[:, O_KR:O_KR + 32], wb.buf)
        kA = sqr[0]
        kB = sqr[1]
        CC = self.alloc([S], F32)
        SS = self.alloc([S], F32)
        P.dma("sp", CC.ap[0:64, :], self.ropeT[0], reads=[self.db("rope", 0)], writes=[CC.buf])
        P.dma("sp", SS.ap[0:64, :], self.ropeT[1], reads=[self.db("rope", 1)], writes=[SS.buf])
        for (wt, dst, tab) in ((wa, kA, CC), (wb, kB, SS)):
            def consume(tb, pa, pbs, dst=dst, tab=tab):
                P.op("dve", lambda e: e.tensor_tensor(out=dst.ap[0:64, tb * 512:(tb + 1) * 512], in0=pa, in1=tab.ap[0:64, tb * 512:(tb + 1) * 512],
                                                      op=ALU.mult), list(pbs) + [tab.buf], [dst.buf])
            self.proj_fm(hT, wt, 64, S, consume)
        k16 = self.alloc([S], BF16)
        P.op("dve", lambda e: e.tensor_tensor(out=k16.ap[0:64, :], in0=kA.ap[0:64, :], in1=kB.ap[0:64, :], op=ALU.add),
             [kA.buf, kB.buf], [k16.buf])
        P.dma("sp", self.krT, k16.ap[0:64, :], reads=[k16.buf], writes=[self.db("krT")])

    def mla_attn(self, l):
        P = self.P
        S = self.S
        NQ = S // 128
        SCALE = 192.0 ** -0.5
        self.stage_reset()
        cqn = self.alloc([4, S], BF16)
        ckvn = self.alloc([4, S], BF16)
        kr16 = self.alloc([S], BF16)
        CC = self.alloc([S], F32)
        SS = self.alloc([S], F32)
        dmask = self.alloc([128], F32)
        P.dma("sp", cqn.ap, self.cqnT[0].rearrange("c p t -> p c t"), reads=[self.db("cqnT", 0)], writes=[cqn.buf])
        P.dma("sp", ckvn.ap, self.cqnT[1].rearrange("c p t -> p c t"), reads=[self.db("cqnT", 1)], writes=[ckvn.buf])
        P.dma("sp", kr16.ap[0:64, :], self.krT, reads=[self.db("krT")], writes=[kr16.buf])
        P.dma("sp", CC.ap[0:64, :], self.ropeT[0], reads=[self.db("rope", 0)], writes=[CC.buf])
        P.dma("sp", SS.ap[0:64, :], self.ropeT[1], reads=[self.db("rope", 1)], writes=[SS.buf])
        P.dma("sp", dmask.ap, self.cst[:, 640:768], writes=[dmask.buf])
        Wkv = self.w_ukv[l]
        Wq = self.w_uq[l]
        wvv = self.alloc([4, 8, 128], BF16)
        for rc in range(4):
            P.dma("pool", wvv.ap[:, rc], Wkv[rc * 128:(rc + 1) * 128, :].rearrange("p (h t k) -> p h t k", t=2, k=128)[:, :, 1, :],
                  writes=[wvv.buf])
        v16 = self.alloc([NQ, 1024], BF16)
        for tt in range(NQ):
            va, vbs = self.bank(2 * (tt % 2), 2)
            for half in range(2):
                for rc in range(4):
                    P.op("pe", lambda e, va=va, half=half, rc=rc, tt=tt: e.matmul(
                        va[:, half * 512:(half + 1) * 512], ckvn.ap[:, rc, tt * 128:(tt + 1) * 128],
                        wvv.ap[:, rc, half * 4:(half + 1) * 4, :].rearrange("p h k -> p (h k)"), start=(rc == 0), stop=(rc == 3)),
                        [ckvn.buf, wvv.buf], [vbs[half]], signal=(rc == 3))
            P.op("act", lambda e, va=va, tt=tt: e.activation(out=v16.ap[:, tt, :], in_=va, func=AF.Copy), vbs, [v16.buf])
        wqr = self.ring(2, [4, 192], BF16)
        wqbr = self.ring(2, [4, 64], BF16)
        wkr = self.ring(2, [4, 128], BF16)
        qn16 = self.alloc([S], BF16)
        kn16 = self.alloc([S], BF16)
        qr16 = self.alloc([S], BF16)
        qA = self.alloc([S], F32)
        qB = self.alloc([S], F32)
        P16 = self.alloc([S], BF16)
        PT16 = self.alloc([16, 128], BF16)
        o16 = self.alloc([128], BF16)
        oT16r = self.ring(2, [512], BF16)
        mx = self.alloc([1], F32)
        nmx = self.alloc([1], F32)
        rsum = self.alloc([1], F32)
        rinv = self.alloc([1], F32)
        ps16 = self.psum.bitcast(BF16)
        for h in range(8):
            wq = wqr[h % 2]
            wqb = wqbr[h % 2]
            wk = wkr[h % 2]
            self.wtile(wq.ap, Wq[:, h * 192:(h + 1) * 192], wq.buf)
            self.wtile(wqb.ap[:, :, 0:32], Wq[:, h * 192 + 160:h * 192 + 192], wqb.buf)
            self.wtile(wqb.ap[:, :, 32:64], Wq[:, h * 192 + 128:h * 192 + 160], wqb.buf)
            self.wtile(wk.ap, Wkv[:, h * 256:h * 256 + 128], wk.buf)

            def c_qn(tb, pa, pbs):
                P.op("act", lambda e: e.activation(out=qn16.ap[:, tb * 512:(tb + 1) * 512], in_=pa, func=AF.Copy, scale=SCALE), pbs, [qn16.buf])

            def c_kn(tb, pa, pbs):
                P.op("act", lambda e: e.activation(out=kn16.ap[:, tb * 512:(tb + 1) * 512], in_=pa, func=AF.Copy), pbs, [kn16.buf])

            def c_qa(tb, pa, pbs):
                P.op("dve", lambda e: e.tensor_tensor(out=qA.ap[0:64, tb * 512:(tb + 1) * 512], in0=pa, in1=CC.ap[0:64, tb * 512:(tb + 1) * 512],
                                                      op=ALU.mult), list(pbs) + [CC.buf], [qA.buf])

            def c_qb(tb, pa, pbs):
                P.op("dve", lambda e: e.tensor_tensor(out=qB.ap[0:64, tb * 512:(tb + 1) * 512], in0=pa, in1=SS.ap[0:64, tb * 512:(tb + 1) * 512],
                                                      op=ALU.mult), list(pbs) + [SS.buf], [qB.buf])
            self.proj_fm(cqn, T(wq.ap[:, :, 0:128], wq.buf), 128, S, c_qn, K=4)
            self.proj_fm(ckvn, wk, 128, S, c_kn, K=4)
            self.proj_fm(cqn, T(wq.ap[:, :, 128:192], wq.buf), 64, S, c_qa, K=4)
            self.proj_fm(cqn, wqb, 64, S, c_qb, K=4)
            P.op("dve", lambda e: e.tensor_tensor(out=qA.ap[0:64, :], in0=qA.ap[0:64, :], in1=qB.ap[0:64, :], op=ALU.add),
                 [qA.buf, qB.buf], [qA.buf])
            P.op("act", lambda e: e.activation(out=qr16.ap[0:64, :], in_=qA.ap[0:64, :], func=AF.Copy, scale=SCALE), [qA.buf], [qr16.buf])
            for qi in range(NQ):
                nk = (qi + 1) * 128
                nkb = (nk + 511) // 512
                sa, sbs = self.bank(0, nkb)
                q0 = qi * 128
                for kb in range(nkb):
                    w = min(512, nk - kb * 512)
                    P.op("pe", lambda e, kb=kb, w=w, q0=q0: e.matmul(sa[:, kb * 512:kb * 512 + w], qn16.ap[:, q0:q0 + 128],
                                                                    kn16.ap[:, kb * 512:kb * 512 + w], start=True, stop=False),
                         [qn16.buf, kn16.buf], [sbs[kb]], signal=False)
                    P.op("pe", lambda e, kb=kb, w=w, q0=q0: e.matmul(sa[:, kb * 512:kb * 512 + w], qr16.ap[0:64, q0:q0 + 128],
                                                                    kr16.ap[0:64, kb * 512:kb * 512 + w], start=False, stop=True),
                         [qr16.buf, kr16.buf], [sbs[kb]], signal=True)
                P.op("dve", lambda e, nk=nk: e.tensor_tensor(out=sa[:, nk - 128:nk], in0=sa[:, nk - 128:nk], in1=dmask.ap, op=ALU.add),
                     [sbs[nkb - 1], dmask.buf], [sbs[nkb - 1]])
                P.op("dve", lambda e, nk=nk: e.reduce_max(out=mx.ap, in_=sa[:, 0:nk], axis=AX.X), list(sbs), [mx.buf])
                P.op("dve", lambda e: e.tensor_scalar(out=nmx.ap, in0=mx.ap, scalar1=-1.0, scalar2=None, op0=ALU.mult), [mx.buf], [nmx.buf])
                P.op("act", lambda e, nk=nk: e.activation(out=P16.ap[:, 0:nk], in_=sa[:, 0:nk], func=AF.Exp, bias=nmx.ap[:, 0:1], scale=1.0,
                                                         accum_out=rsum.ap[:, 0:1]), list(sbs) + [nmx.buf], [P16.buf, rsum.buf])
                P.op("dve", lambda e: e.reciprocal(out=rinv.ap, in_=rsum.ap), [rsum.buf], [rinv.buf])
                nblk = nk // 128
                for kb in range(nblk):
                    bk = 4 + kb // 8
                    P.op("pe", lambda e, kb=kb, bk=bk: e.transpose(ps16[:, bk * 1024 + (kb % 8) * 128:bk * 1024 + (kb % 8 + 1) * 128],
                                                                  P16.ap[:, kb * 128:(kb + 1) * 128], self.ident16.ap),
                         [P16.buf, self.ident16.buf], [self.pb[bk]], signal=(kb % 8 == 7 or kb == nblk - 1))
                for bi in range((nblk + 7) // 8):
                    n8 = min(8, nblk - bi * 8)
                    src = ps16[:, (4 + bi) * 1024:(4 + bi) * 1024 + n8 * 128]
                    dst = PT16.ap[:, bi * 8:bi * 8 + n8, :].rearrange("p a b -> p (a b)")
                    if bi == 0:
                        P.op("act", lambda e, src=src, dst=dst: e.activation(out=dst, in_=src, func=AF.Copy), [self.pb[4 + bi]], [PT16.buf])
                    else:
                        P.op("dve", lambda e, src=src, dst=dst: e.tensor_copy(out=dst, in_=src), [self.pb[4 + bi]], [PT16.buf])
                oa, obs = self.bank(6)
                for kb in range(nblk):
                    P.op("pe", lambda e, kb=kb, h=h: e.matmul(oa[:, 0:128], PT16.ap[:, kb, :], v16.ap[:, kb, h * 128:(h + 1) * 128],
                                                             start=(kb == 0), stop=(kb == nblk - 1)),
                         [PT16.buf, v16.buf], obs, signal=(kb == nblk - 1))
                P.op("act", lambda e: e.activation(out=o16.ap, in_=oa[:, 0:128], func=AF.Identity, scale=rinv.ap[:, 0:1]),
                     list(obs) + [rinv.buf], [o16.buf])
                P.op("pe", lambda e, qi=qi: e.transpose(ps16[:, 7 * 1024 + (qi % 4) * 128:7 * 1024 + (qi % 4 + 1) * 128], o16.ap, self.ident16.ap),
                     [o16.buf, self.ident16.buf], [self.pb[7]])
                if qi % 4 == 3:
                    oT = oT16r[(qi // 4) % 2]
                    P.op("dve", lambda e, oT=oT: e.tensor_copy(out=oT.ap, in_=ps16[:, 7 * 1024:7 * 1024 + 512]), [self.pb[7]], [oT.buf])
                    P.dma("sp", self.obT[2, h, :, (qi // 4) * 512:(qi // 4 + 1) * 512], oT.ap, reads=[oT.buf],
                          writes=[self.db("obT", 2, qi // 4)])

    def dn_proj(self, l, hT):
        P = self.P
        S = self.S
        NT = S // 128
        W = self.w_in[l]
        if not hasattr(self, "qkvzT"):
            self.qkvzT = self.nc.dram_tensor("qkvzT", [32, 128, S], F32).ap()
            self.gbD = self.nc.dram_tensor("gbD", [128, NT * 16], F32).ap()
        wr = self.ring(3, [DC * 128], BF16)
        outr = self.ring(3, [S], F32)
        for t in range(32):
            wt = wr[t % 3]
            wt3 = T(wt.ap.rearrange("p (c k) -> p c k", k=128), wt.buf)
            self.wtile(wt3.ap, W[:, t * 128:(t + 1) * 128], wt.buf)
            o = outr[t % 3]

            def consume(tb, pa, pbs, o=o, t=t):
                P.op("act", lambda e: e.activation(out=o.ap[:, tb * 512:(tb + 1) * 512], in_=pa, func=(AF.Silu if t >= 24 else AF.Copy)),
                     pbs, [o.buf])
            self.proj_fm(hT, wt3, 128, S, consume)
            P.dma("sp", self.qkvzT[t], o.ap, reads=[o.buf], writes=[self.db("qkvz", t)])
        wab = self.alloc([DC, 16], BF16)
        self.wtile(wab.ap, W[:, O_A:O_A + 16], wab.buf)
        pa, pbs = self.bank(4)
        for tt in range(NT):
            for dc in range(DC):
                P.op("pe", lambda e, tt=tt, dc=dc: e.matmul(pa[:, tt * 16:(tt + 1) * 16], hT.ap[:, dc, tt * 128:(tt + 1) * 128], wab.ap[:, dc, :],
                                                           start=(dc == 0), stop=(dc == DC - 1)),
                     [hT.buf, wab.buf], pbs, signal=(dc == DC - 1))
        ab = pa[:, 0:NT * 16].rearrange("p (t k) -> p t k", k=16)
        dtb = self.alloc([8], F32)
        alg = self.alloc([8], F32)
        P.dma("sp", dtb.ap, self.dt_bias[l:l + 1, :].broadcast_to([128, 8]), writes=[dtb.buf])
        P.dma("sp", alg.ap, self.a_log[l:l + 1, :].broadcast_to([128, 8]), writes=[alg.buf])
        P.op("act", lambda e: e.activation(out=alg.ap, in_=alg.ap, func=AF.Exp), [alg.buf], [alg.buf])
        bc = lambda t: t.ap.unsqueeze(1).broadcast_to([128, NT, 8])
        x = self.alloc([NT, 8], F32)
        ax = self.alloc([NT, 8], F32)
        gb = self.alloc([NT, 16], F32)
        P.op("dve", lambda e: e.tensor_tensor(out=x.ap, in0=ab[:, :, 0:8], in1=bc(dtb), op=ALU.add), list(pbs) + [dtb.buf], [x.buf])
        P.op("dve", lambda e: e.scalar_tensor_tensor(out=ax.ap, in0=x.ap, scalar=-1.0, in1=x.ap, op0=ALU.mult, op1=ALU.max), [x.buf], [ax.buf])
        P.op("act", lambda e: e.activation(out=ax.ap, in_=ax.ap, func=AF.Exp, scale=-1.0), [ax.buf], [ax.buf])
        P.op("act", lambda e: e.activation(out=ax.ap, in_=ax.ap, func=AF.Ln, bias=self.ones32.ap[:, 0:1], scale=1.0), [ax.buf, self.ones32.buf], [ax.buf])
        P.op("dve", lambda e: e.scalar_tensor_tensor(out=x.ap, in0=x.ap, scalar=0.0, in1=ax.ap, op0=ALU.max, op1=ALU.add), [x.buf, ax.buf], [x.buf])
        P.op("dve", lambda e: e.scalar_tensor_tensor(out=gb.ap[:, :, 0:8], in0=x.ap, scalar=-1.0, in1=bc(alg), op0=ALU.mult, op1=ALU.mult),
             [x.buf, alg.buf], [gb.buf])
        P.op("act", lambda e: e.activation(out=gb.ap[:, :, 8:16], in_=ab[:, :, 8:16], func=AF.Sigmoid), pbs, [gb.buf])
        P.dma("sp", self.gbD, gb.ap.rearrange("p t k -> p (t k)"), reads=[gb.buf], writes=[self.db("gbD")])

    def dn_core(self, l):
        P = self.P
        S = self.S
        NT = S // 128
        nb = S // 512
        self.stage_reset()
        cw = self.cols_from_rows(self.conv_w[l], 4, 24)
        dng = self.cols_from_rows(self.dn_g[l:l + 1, :].broadcast_to([2, 128]), 2, 1)
        Ms = self.alloc([128], F32)
        U = self.alloc([128], F32)
        P.dma("sp", Ms.ap, self.cst[:, 256:384], writes=[Ms.buf])
        P.dma("sp", U.ap, self.cst[:, 384:512], writes=[U.buf])
        gb = self.alloc([NT, 16], F32)
        P.dma("sp", gb.ap.rearrange("p t k -> p (t k)"), self.gbD, reads=[self.db("gbD")], writes=[gb.buf])
        I32_ = self.ident32
        ones = self.ones32
        gall = gb.ap[:, :, 0:8]
        ball = gb.ap[:, :, 8:16]
        GC = self.alloc([NT, 8], F32)
        egc = self.alloc([NT, 8], F32)
        dtl = self.alloc([NT, 8], F32)
        egl = self.alloc([NT, 8], F32)
        bge = self.alloc([NT, 8], F32)
        nbt = self.alloc([NT, 8], F32)
        pa6, pb6 = self.bank(6)
        pa7, pb7 = self.bank(7)
        v3 = lambda ap: ap.rearrange("p (t k) -> p t k", k=8)
        P.op("pe", lambda e: e.matmul(pa6[:, 0:NT * 8], U.ap, gall, start=True, stop=True), [U.buf, gb.buf], pb6)
        P.op("pe", lambda e: e.matmul(pa7[:, 0:NT * 8], ones.ap, gall, start=True, stop=True), [ones.buf, gb.buf], pb7)
        P.op("act", lambda e: e.activation(out=GC.ap, in_=v3(pa6[:, 0:NT * 8]), func=AF.Copy), pb6, [GC.buf])
        P.op("act", lambda e: e.activation(out=egc.ap, in_=v3(pa6[:, 0:NT * 8]), func=AF.Exp), pb6, [egc.buf])
        P.op("act", lambda e: e.activation(out=egl.ap, in_=v3(pa7[:, 0:NT * 8]), func=AF.Exp), pb7, [egl.buf])
        P.op("dve", lambda e: e.tensor_tensor(out=dtl.ap, in0=v3(pa7[:, 0:NT * 8]), in1=GC.ap, op=ALU.subtract), list(pb7) + [GC.buf], [dtl.buf])
        P.op("act", lambda e: e.activation(out=dtl.ap, in_=dtl.ap, func=AF.Exp), [dtl.buf], [dtl.buf])
        P.op("dve", lambda e: e.tensor_tensor(out=bge.ap, in0=egc.ap, in1=ball, op=ALU.mult), [egc.buf, gb.buf], [bge.buf])
        P.op("dve", lambda e: e.tensor_scalar(out=nbt.ap, in0=ball, scalar1=-1.0, scalar2=None, op0=ALU.mult), [gb.buf], [nbt.buf])

        xp = self.alloc([S + 4], F32)
        acc = self.alloc([S], F32)
        sq = self.alloc([S], F32)
        rs = self.alloc([S], F32)
        qT = self.alloc([S], F32)
        kT = self.alloc([S], F32)
        vT = self.alloc([S], F32)
        ktm = self.alloc([NT, 128], F32)
        vtm = self.alloc([NT, 128], F32)
        wTs = self.alloc([NT, 128], F32)
        us = self.alloc([NT, 128], F32)
        qgs = self.alloc([NT, 128], F32)
        qks = self.alloc([NT, 128], F32)
        kds = self.alloc([NT, 128], F32)
        oT = self.alloc([S], F32)
        zs = self.alloc([S], F32)
        o16 = self.alloc([S], BF16)
        St = self.alloc([128], F32)
        vnew = self.alloc([128], F32)
        sm = lambda: self.alloc([128], F32)
        gB, nD, Xp, E, ET, EG, Nk, NTk, TT, kbg, vb = (sm() for _ in range(11))
        Nk2, NTk2 = sm(), sm()
        P.op("pool", lambda e: e.memset(xp.ap[:, 0:3], 0.0), [], [xp.buf])
        for h in range(8):
            for wi, dst in ((0, qT), (1, kT), (2, vT)):
                t = wi * 8 + h
                P.dma("sp", xp.ap[:, 3:3 + S], self.qkvzT[t], reads=[self.db("qkvz", t)], writes=[xp.buf])
                P.op("act", lambda e, t=t: e.activation(out=acc.ap, in_=xp.ap[:, 3:3 + S], func=AF.Identity, scale=cw.ap[:, t, 3:4]),
                     [xp.buf, cw.buf], [acc.buf])
                for k in range(3):
                    P.op("dve", lambda e, t=t, k=k: e.scalar_tensor_tensor(out=acc.ap, in0=xp.ap[:, k:k + S], scalar=cw.ap[:, t, k:k + 1],
                                                                          in1=acc.ap, op0=ALU.mult, op1=ALU.add),
                         [xp.buf, cw.buf, acc.buf], [acc.buf])
                if wi == 2:
                    P.op("act", lambda e, dst=dst: e.activation(out=dst.ap, in_=acc.ap, func=AF.Silu), [acc.buf], [dst.buf])
                    continue
                P.op("act", lambda e: e.activation(out=acc.ap, in_=acc.ap, func=AF.Silu), [acc.buf], [acc.buf])
                P.op("act", lambda e: e.activation(out=sq.ap, in_=acc.ap, func=AF.Square), [acc.buf], [sq.buf])
                sa, sbs = self.bank(0, nb)
                for j in range(nb):
                    P.op("pe", lambda e, j=j: e.matmul(sa[:, j * 512:(j + 1) * 512], ones.ap, sq.ap[:, j * 512:(j + 1) * 512], start=True, stop=True),
                         [sq.buf, ones.buf], [sbs[j]])
                self.rstd_from(sa, sbs, rs, 1)
                P.op("dve", lambda e, dst=dst, wi=wi: e.scalar_tensor_tensor(out=dst.ap, in0=acc.ap, scalar=(128.0 ** -0.5 if wi == 0 else 1.0),
                                                                            in1=rs.ap, op0=ALU.mult, op1=ALU.mult),
                     [acc.buf, rs.buf], [dst.buf])
            P.dma("sp", zs.ap, self.qkvzT[24 + h], reads=[self.db("qkvz", 24 + h)], writes=[zs.buf])
            for src, dst in ((kT, ktm), (vT, vtm)):
                for c4 in range(NT // 4):
                    pa, pbs = self.bank(c4 % 2)
                    for j in range(4):
                        c = c4 * 4 + j
                        P.op("pe", lambda e, pa=pa, j=j, c=c, src=src: e.transpose(pa[:, j * 128:(j + 1) * 128], src.ap[:, c * 128:(c + 1) * 128], I32_.ap),
                             [src.buf, I32_.buf], pbs, signal=(j == 3))
                    P.op("act", lambda e, pa=pa, c4=c4, dst=dst: e.activation(out=dst.ap[:, c4 * 4:(c4 + 1) * 4, :].rearrange("p a b -> p (a b)"),
                                                                          in_=pa, func=AF.Copy), pbs, [dst.buf])
            for c in range(NT):
                cs = slice(c * 128, (c + 1) * 128)
                col = lambda t: t.ap[:, c, h:h + 1]
                p0, b0 = self.bank(0)
                p1, b1 = self.bank(1 + c % 2)
                p3, b3 = self.bank(3)
                KK, KQ, GR, NTp = p0[:, 0:128], p0[:, 128:256], p0[:, 256:384], p0[:, 384:512]
                P.op("pe", lambda e, cs=cs, KK=KK: e.matmul(KK, kT.ap[:, cs], kT.ap[:, cs], start=True, stop=True), [kT.buf], b0)
                P.op("pe", lambda e, cs=cs, KQ=KQ: e.matmul(KQ, kT.ap[:, cs], qT.ap[:, cs], start=True, stop=True), [kT.buf, qT.buf], b0)
                P.op("dve", lambda e, c=c, h=h: e.tensor_scalar(out=gB.ap, in0=ones.ap, scalar1=gb.ap[:, c, h:h + 1], scalar2=None, op0=ALU.mult),
                     [ones.buf, gb.buf], [gB.buf])
                P.op("pe", lambda e, GR=GR: e.matmul(GR, gB.ap, U.ap, start=True, stop=True), [gB.buf, U.buf], b0)
                P.op("dve", lambda e, GR=GR, c=c, h=h: e.tensor_scalar(out=nD.ap, in0=GR, scalar1=GC.ap[:, c, h:h + 1], scalar2=0.0,
                                                                    op0=ALU.subtract, op1=ALU.max), list(b0) + [GC.buf], [nD.buf])
                P.op("dve", lambda e, GR=GR, c=c, h=h: e.tensor_scalar(out=Xp.ap, in0=GR, scalar1=GC.ap[:, c, h:h + 1], scalar2=0.0,
                                                                    op0=ALU.subtract, op1=ALU.min), list(b0) + [GC.buf], [Xp.buf])
                P.op("act", lambda e: e.activation(out=E.ap, in_=nD.ap, func=AF.Exp, scale=-1.0), [nD.buf], [E.buf])
                P.op("act", lambda e: e.activation(out=ET.ap, in_=Xp.ap, func=AF.Exp), [Xp.buf], [ET.buf])
                P.op("act", lambda e, GR=GR: e.activation(out=EG.ap, in_=GR, func=AF.Exp), b0, [EG.buf])
                P.op("pool", lambda e: e.tensor_tensor(out=E.ap, in0=E.ap, in1=Ms.ap, op=ALU.mult), [E.buf, Ms.buf], [E.buf])
                P.op("pool", lambda e: e.tensor_tensor(out=ET.ap, in0=ET.ap, in1=U.ap, op=ALU.mult), [ET.buf, U.buf], [ET.buf])
                P.op("dve", lambda e, KK=KK, c=c, h=h: e.scalar_tensor_tensor(out=Nk.ap, in0=KK, scalar=nbt.ap[:, c, h:h + 1], in1=E.ap,
                                                                           op0=ALU.mult, op1=ALU.mult), list(b0) + [nbt.buf, E.buf], [Nk.buf])
                P.op("dve", lambda e, KQ=KQ, c=c: e.tensor_tensor(out=qks.ap[:, c, :], in0=KQ, in1=ET.ap, op=ALU.mult), list(b0) + [ET.buf], [qks.buf])
                P.op("pool", lambda e, cs=cs, c=c: e.tensor_tensor(out=qgs.ap[:, c, :], in0=qT.ap[:, cs], in1=EG.ap, op=ALU.mult),
                     [qT.buf, EG.buf], [qgs.buf])
                P.op("pe", lambda e, NTp=NTp: e.transpose(NTp, Nk.ap, I32_.ap), [Nk.buf, I32_.buf], b0)
                P.op("act", lambda e, NTp=NTp: e.activation(out=NTk.ap, in_=NTp, func=AF.Copy), b0, [NTk.buf])
                P.op("dve", lambda e, NTp=NTp: e.tensor_tensor(out=TT.ap, in0=NTp, in1=I32_.ap, op=ALU.add), list(b0) + [I32_.buf], [TT.buf])
                cur, curT, nxt, nxtT = Nk, NTk, Nk2, NTk2
                for lev in range(6):
                    A2, AT2, TU = p1[:, 0:128], p1[:, 128:256], p1[:, 256:384]
                    P.op("pe", lambda e, A2=A2, cur=cur, curT=curT: e.matmul(A2, curT.ap, cur.ap, start=True, stop=True), [cur.buf, curT.buf], b1)
                    if lev < 5:
                        P.op("pe", lambda e, AT2=AT2, cur=cur, curT=curT: e.matmul(AT2, cur.ap, curT.ap, start=True, stop=True),
                             [cur.buf, curT.buf], b1)
                    P.op("act", lambda e, A2=A2, nxt=nxt: e.activation(out=nxt.ap, in_=A2, func=AF.Copy), b1, [nxt.buf])
                    if lev < 5:
                        P.op("dve", lambda e, AT2=AT2, nxtT=nxtT: e.tensor_copy(out=nxtT.ap, in_=AT2), b1, [nxtT.buf])
                    P.op("pe", lambda e, TU=TU, nxt=nxt: e.matmul(TU, nxt.ap, TT.ap, start=True, stop=True), [nxt.buf, TT.buf], b1)
                    P.op("dve", lambda e, TU=TU: e.tensor_tensor(out=TT.ap, in0=TU, in1=TT.ap, op=ALU.add), list(b1) + [TT.buf], [TT.buf])
                    cur, curT, nxt, nxtT = nxt, nxtT, cur, curT
                P.op("dve", lambda e, c=c, h=h: e.tensor_scalar(out=kbg.ap, in0=ktm.ap[:, c, :], scalar1=bge.ap[:, c, h:h + 1], scalar2=None, op0=ALU.mult),
                     [ktm.buf, bge.buf], [kbg.buf])
                P.op("pool", lambda e, c=c, h=h: e.tensor_scalar(out=vb.ap, in0=vtm.ap[:, c, :], scalar1=gb.ap[:, c, 8 + h:9 + h], scalar2=None, op0=ALU.mult),
                     [vtm.buf, gb.buf], [vb.buf])
                P.op("pool", lambda e, c=c, h=h: e.tensor_scalar(out=kds.ap[:, c, :], in0=ktm.ap[:, c, :], scalar1=dtl.ap[:, c, h:h + 1], scalar2=None,
                                                                op0=ALU.mult), [ktm.buf, dtl.buf], [kds.buf])
                P.op("pe", lambda e: e.matmul(p3[:, 0:128], kbg.ap, TT.ap, start=True, stop=True), [kbg.buf, TT.buf], b3)
                P.op("pe", lambda e: e.matmul(p3[:, 128:256], TT.ap, vb.ap, start=True, stop=True), [vb.buf, TT.buf], b3)
                P.op("act", lambda e, c=c: e.activation(out=wTs.ap[:, c, :], in_=p3[:, 0:128], func=AF.Copy), b3, [wTs.buf])
                P.op("dve", lambda e, c=c: e.tensor_copy(out=us.ap[:, c, :], in_=p3[:, 128:256]), b3, [us.buf])
            P.op("pool", lambda e: e.memset(St.ap, 0.0), [], [St.buf])
            p4, b4 = self.bank(4)
            p5, b5 = self.bank(5)
            for c in range(NT):
                P.op("pe", lambda e, c=c: e.matmul(p4[:, 0:128], wTs.ap[:, c, :], St.ap, start=True, stop=True), [wTs.buf, St.buf], b4)
                P.op("dve", lambda e, c=c: e.tensor_tensor(out=vnew.ap, in0=us.ap[:, c, :], in1=p4[:, 0:128], op=ALU.subtract),
                     [us.buf] + list(b4), [vnew.buf])
                P.op("pe", lambda e, c=c: e.matmul(p5[:, 0:128], St.ap, qgs.ap[:, c, :], start=True, stop=False), [St.buf, qgs.buf], b5, signal=False)
                P.op("pe", lambda e, c=c: e.matmul(p5[:, 0:128], vnew.ap, qks.ap[:, c, :], start=False, stop=True), [vnew.buf, qks.buf], b5)
                P.op("pe", lambda e, c=c: e.matmul(p4[:, 128:256], kds.ap[:, c, :], vnew.ap, start=True, stop=True), [kds.buf, vnew.buf], b4)
                P.op("act", lambda e, c=c: e.activation(out=oT.ap[:, c * 128:(c + 1) * 128], in_=p5[:, 0:128], func=AF.Copy), b5, [oT.buf])
                P.op("dve", lambda e, c=c, h=h: e.scalar_tensor_tensor(out=St.ap, in0=St.ap, scalar=egl.ap[:, c, h:h + 1], in1=p4[:, 128:256],
                                                                      op0=ALU.mult, op1=ALU.add), [St.buf, egl.buf] + list(b4), [St.buf])
            P.op("act", lambda e: e.activation(out=sq.ap, in_=oT.ap, func=AF.Square), [oT.buf], [sq.buf])
            sa, sbs = self.bank(0, nb)
            for j in range(nb):
                P.op("pe", lambda e, j=j: e.matmul(sa[:, j * 512:(j + 1) * 512], ones.ap, sq.ap[:, j * 512:(j + 1) * 512], start=True, stop=True),
                     [sq.buf, ones.buf], [sbs[j]])
            self.rstd_from(sa, sbs, rs, 128)
            P.op("dve", lambda e: e.scalar_tensor_tensor(out=oT.ap, in0=oT.ap, scalar=dng.ap[:, 0, 0:1], in1=rs.ap, op0=ALU.mult, op1=ALU.mult),
                 [oT.buf, dng.buf, rs.buf], [oT.buf])
            P.op("pool", lambda e: e.tensor_tensor(out=o16.ap, in0=oT.ap, in1=zs.ap, op=ALU.mult), [oT.buf, zs.buf], [o16.buf])
            P.dma("sp", self.obT[0, h], o16.ap, reads=[o16.buf], writes=[self.db("obT", 0, j) for j in range(nb)])


def host_consts():
    c = np.zeros((128, 1024), np.float32)
    i = np.arange(128)[:, None]
    j = np.arange(128)[None, :]
    c[:, 0:128] = np.eye(128, dtype=np.float32)
    c[:, 128:256] = 1.0
    c[:, 256:384] = (i > j)
    c[:, 384:512] = (i <= j)
    c[:, 640:768] = np.where((i < 64) & (j >= 64), -30000.0, 0.0)
    inv_freq = np.power(np.float32(10000.0), -np.arange(0, 64, 2, dtype=np.float32) / np.float32(64)).astype(np.float32)
    c[0:64, 768] = np.concatenate([inv_freq, inv_freq])
    c[0:32, 769] = -1.0
    c[32:64, 769] = 1.0
    c[:, 770:898] = np.where((i >= 64) & (j < 64), 0.0, 1.0)
    return c


_CACHE = {}


def kernel(**inputs):
    x = np.asarray(inputs["x"], np.float32)
    B, S, _ = x.shape
    if "nc" not in _CACHE:
        _CACHE["nc"] = KB(S).build()
    nc = _CACHE["nc"]
    shared = {k: np.ascontiguousarray(np.asarray(v)) for k, v in inputs.items() if k not in ("x", "positions")}
    shared["consts"] = host_consts()
    pos = np.asarray(inputs["positions"], np.int32)
    in_maps = []
    for c in range(8):
        b = c % B
        m = dict(shared)
        m["x"] = np.ascontiguousarray(x[b])
        m["positions"] = np.ascontiguousarray(pos[b:b + 1])
        in_maps.append(m)
    res = run_bass_kernel_spmd(nc, in_maps, core_ids=list(range(8)))
    return np.stack([np.asarray(res.results[b]["out"], np.float32) for b in range(B)], axis=0)
```

```python
from contextlib import ExitStack

import numpy as np
import concourse.bass as bass
import concourse.mybir as mybir
from concourse.bass_utils import run_bass_kernel_spmd

F32 = mybir.dt.float32
BF16 = mybir.dt.bfloat16
I32 = mybir.dt.int32
AF = mybir.ActivationFunctionType
ALU = mybir.AluOpType
AX = mybir.AxisListType

D = 2048
DC = 16
DFF = 5632
NFT = 44
DEPTH = 2
NH = 8
IN_COLS = 13392
O_QKV = 0
O_Z = 3072
O_A = 4096
O_B = 4104
O_SG = 4112
O_CQ = 6160
O_CKV = 6672
O_KR = 7184
O_GATE = 7248
EPS = 1e-6

ENGS = ("pe", "act", "dve", "pool", "sp")


class Buf:
    __slots__ = ("name", "last_write", "reads", "excl")

    def __init__(self, name="", excl=False):
        self.name = name
        self.last_write = None
        self.reads = []
        self.excl = excl


class Op:
    __slots__ = ("eng", "fn", "deps", "is_dma", "idx", "sem_key", "sem_val", "signal", "clock")

    def __init__(self, eng, fn, is_dma):
        self.eng = eng
        self.fn = fn
        self.is_dma = is_dma
        self.deps = []
        self.signal = True
        self.clock = None


class Prog:
    def __init__(self, nc, n_dma_sems=16):
        self.nc = nc
        self.ops = []
        self.n_dma_sems = n_dma_sems
        self.stack = ExitStack()
        self.G = Buf("G")

    def op(self, eng, fn, reads=(), writes=(), is_dma=False, signal=True, barrier=False):
        o = Op(eng, fn, is_dma)
        o.signal = signal
        writes = list(writes) + [b for b in reads if b.excl]
        reads = [b for b in reads if not b.excl]
        if barrier:
            writes.append(self.G)
        else:
            reads.append(self.G)
        deps = []
        for b in reads:
            if b.last_write is not None:
                deps.append(b.last_write)
        for b in writes:
            if b.last_write is not None:
                deps.append(b.last_write)
            deps.extend(b.reads)
        for b in reads:
            b.reads.append(o)
        for b in writes:
            b.last_write = o
            b.reads = []
        seen = set()
        for d in deps:
            if id(d) not in seen and d is not o:
                seen.add(id(d))
                o.deps.append(d)
        o.idx = len(self.ops)
        self.ops.append(o)
        return o

    def dma(self, queue, out, in_, reads=(), writes=(), **kw):
        return self.op(queue, lambda e: e.dma_start(out=out, in_=in_, **kw), reads, writes, is_dma=True)

    def emit(self):
        nc = self.nc
        st = self.stack
        per_eng = {e: [] for e in ENGS}
        for o in self.ops:
            per_eng[o.eng].append(o)
        esem = {e: st.enter_context(nc.semaphore(f"s_{e}")) for e in ENGS}
        dsem = {e: [st.enter_context(nc.semaphore(f"d_{e}{i}")) for i in range(self.n_dma_sems)]
                for e in ("sp", "act", "pool")}
        nxt_sig = {}
        for e in ENGS:
            nxt = None
            for o in reversed(per_eng[e]):
                if o.is_dma or o.signal:
                    nxt = o
                nxt_sig[id(o)] = nxt
        for o in self.ops:
            nd = []
            for d in o.deps:
                if not d.is_dma and not d.signal:
                    r = nxt_sig[id(d)]
                    if r is None or r.idx >= o.idx or r.is_dma:
                        d.signal = True
                        nd.append(d)
                    else:
                        nd.append(r)
                else:
                    nd.append(d)
            o.deps = nd
        cnt = {e: 0 for e in ENGS}
        dcnt = {e: 0 for e in ("sp", "act", "pool")}
        dval = {}
        dlast = {}
        for o in self.ops:
            if o.is_dma:
                i = dcnt[o.eng] % self.n_dma_sems
                dcnt[o.eng] += 1
                key = ("d", o.eng, i)
                prev = dlast.get(key)
                if prev is not None:
                    o.deps.append(prev)
                dlast[key] = o
                dval[key] = dval.get(key, 0) + 16
                o.sem_key = key
                o.sem_val = dval[key]
            elif o.signal:
                cnt[o.eng] += 1
                o.sem_key = ("e", o.eng)
                o.sem_val = cnt[o.eng]
            else:
                o.sem_key = None
                o.sem_val = 0
        known = {e: {} for e in ENGS}
        waits = {}
        for o in self.ops:
            k = known[o.eng]
            w = {}
            for d in o.deps:
                if d.eng == "pe" and o.eng == "pe" and not d.is_dma:
                    continue
                if k.get(d.sem_key, 0) >= d.sem_val:
                    continue
                if w.get(d.sem_key, 0) < d.sem_val:
                    w[d.sem_key] = d.sem_val
            for d in o.deps:
                if d.sem_key in w and d.clock:
                    for kk, vv in d.clock.items():
                        if k.get(kk, 0) < vv:
                            k[kk] = vv
            for kk, vv in w.items():
                if k.get(kk, 0) < vv:
                    k[kk] = vv
            waits[id(o)] = w
            clk = dict(k)
            if o.sem_key is not None:
                clk[o.sem_key] = o.sem_val
            o.clock = clk

        def sem_of(key):
            if key[0] == "e":
                return esem[key[1]]
            return dsem[key[1]][key[2]]

        def run(e, eng):
            for o in per_eng[e]:
                for kk, vv in waits[id(o)].items():
                    eng.wait_ge(sem_of(kk), vv)
                ins = o.fn(eng)
                if o.is_dma:
                    ins.then_inc(sem_of(o.sem_key), 16)
                elif o.signal:
                    ins.then_inc(sem_of(o.sem_key), 1)

        block = st.enter_context(nc.Block())

        @block.tensor
        def _(eng):
            run("pe", eng)

        @block.scalar
        def _(eng):
            run("act", eng)

        @block.vector
        def _(eng):
            run("dve", eng)

        @block.gpsimd
        def _(eng):
            run("pool", eng)

        @block.sync
        def _(eng):
            run("sp", eng)
            for key, v in dval.items():
                eng.wait_ge(sem_of(key), v)

        st.close()
        self.stats = {e: len(per_eng[e]) for e in ENGS}
        return nc


class T:
    __slots__ = ("ap", "buf")

    def __init__(self, ap, buf=None):
        self.ap = ap
        self.buf = buf or Buf()


ARENA_W = 49 * 1024


class KB:
    def __init__(self, S, depth=DEPTH, stages="all", debug=False):
        self.S = S
        self.debug = debug
        self.depth = depth
        self.stages = stages
        nc = self.nc = bass.Bass("TRN2", target_bir_lowering=False)
        P = self.P = Prog(nc)
        st = P.stack
        self.arena = st.enter_context(nc.sbuf_tensor("arena", [128, ARENA_W], F32))
        self.psum = st.enter_context(nc.psum_tensor("psum", [128, 4096], F32))
        self.pb = [Buf(f"ps{i}", excl=True) for i in range(8)]
        self.off = 0
        self.dbufs = {}

        def dram_in(name, shape, dt=F32):
            return nc.dram_tensor(name, list(shape), dt, kind="ExternalInput").ap()

        self.x_in = dram_in("x", [S, D])
        self.pos_in = dram_in("positions", [1, S], I32)
        self.norm_g = dram_in("norm_g", [DEPTH, 6, D])
        self.w_gate = dram_in("ffn_w_gate", [DEPTH, 2, D, DFF])
        self.w_up = dram_in("ffn_w_up", [DEPTH, 2, D, DFF])
        self.w_down = dram_in("ffn_w_down", [DEPTH, 2, DFF, D])
        self.w_in = dram_in("w_in", [DEPTH, D, IN_COLS])
        self.conv_w = dram_in("dn_conv_w", [DEPTH, 4, 3072])
        self.a_log = dram_in("dn_a_log", [DEPTH, 8])
        self.dt_bias = dram_in("dn_dt_bias", [DEPTH, 8])
        self.dn_g = dram_in("dn_norm_g", [DEPTH, 128])
        self.ln_g = dram_in("sg_ln_g", [DEPTH, 1024])
        self.ln_b = dram_in("sg_ln_b", [DEPTH, 1024])
        self.sg_w = dram_in("sg_w", [DEPTH, 8, 128, 128])
        self.sg_b = dram_in("sg_b", [DEPTH, 8, 128])
        self.cq_g = dram_in("mla_cq_norm_g", [DEPTH, 512])
        self.ckv_g = dram_in("mla_ckv_norm_g", [DEPTH, 512])
        self.w_uq = dram_in("mla_w_uq", [DEPTH, 512, 1536])
        self.w_ukv = dram_in("mla_w_ukv", [DEPTH, 512, 2048])
        self.w_branch = dram_in("w_branch", [DEPTH, 3, 1024, D])
        self.w_out = dram_in("w_out", [DEPTH, D, D])
        self.cst = dram_in("consts", [128, 1024])
        self.out = nc.dram_tensor("out", [S, D], F32, kind="ExternalOutput").ap()
        self.xT = nc.dram_tensor("xT", [DC, 128, S], F32).ap()
        self.fT = nc.dram_tensor("fT", [DC, 128, S], F32).ap()
        self.obT = (nc.dram_tensor("obT", [3, 8, 128, S], BF16, kind="ExternalOutput") if debug else
                    nc.dram_tensor("obT", [3, 8, 128, S], BF16)).ap()

        self.ident32 = self.alloc([128], F32)
        self.ones32 = self.alloc([128], F32)
        self.ident16 = self.alloc([128], BF16)
        self.epsc = self.alloc([1], F32)
        self.gcols = self.alloc([DEPTH * 6, DC], F32)
        self.persist = self.off
        self.setup_consts()

    def alloc(self, shape, dt, name=""):
        n = 1
        for s in shape:
            n *= s
        nbytes = n * (4 if dt in (F32, I32) else 2)
        words = (nbytes + 3) // 4
        words = (words + 7) // 8 * 8
        assert self.off + words <= ARENA_W, f"SBUF arena overflow {self.off}+{words}"
        ap = self.arena[:, self.off:self.off + words]
        self.off += words
        if dt != F32:
            ap = ap.bitcast(dt)
        ap = ap[:, 0:n]
        if len(shape) == 2:
            ap = ap.rearrange("p (a b) -> p a b", b=shape[1])
        elif len(shape) == 3:
            ap = ap.rearrange("p (a b c) -> p a b c", b=shape[1], c=shape[2])
        return T(ap)

    def ring(self, n, shape, dt):
        return [self.alloc(shape, dt) for _ in range(n)]

    def bank(self, i, n=1):
        return self.psum[:, 512 * i:512 * (i + n)], self.pb[i:i + n]

    def db(self, *key):
        b = self.dbufs.get(key)
        if b is None:
            b = self.dbufs[key] = Buf(str(key))
        return b

    def barrier(self):
        t = self.epsc
        self.P.op("pool", lambda e: e.memset(t.ap[:, 0:1], EPS), [], [t.buf], barrier=True)

    def stage_reset(self):
        self.barrier()
        self.off = self.persist

    def setup_consts(self):
        P = self.P
        c = self.cst
        P.dma("sp", self.ident32.ap, c[:, 0:128], writes=[self.ident32.buf])
        P.dma("sp", self.ones32.ap, c[:, 128:256], writes=[self.ones32.buf])
        P.dma("pool", self.ident16.ap, c[:, 0:128], writes=[self.ident16.buf])
        self.barrier()
        rows = self.alloc([D], F32)
        R = DEPTH * 6
        P.dma("sp", rows.ap[0:R, :], self.norm_g.rearrange("l k d -> (l k) d"), writes=[rows.buf])
        pa, pbuf = self.bank(0)
        for dc in range(DC):
            P.op("pe", lambda e, dc=dc: e.transpose(pa[:, dc * R:(dc + 1) * R], rows.ap[0:R, dc * 128:(dc + 1) * 128],
                                                   self.ident32.ap[0:R, 0:R]),
                 [rows.buf, self.ident32.buf], pbuf)
        g = self.gcols
        P.op("dve", lambda e: e.tensor_copy(out=g.ap.rearrange("p r c -> p c r"),
                                            in_=pa[:, 0:DC * R].rearrange("p (c r) -> p c r", r=R)),
             pbuf, [g.buf])
        self.off = self.persist

    def gcol(self, l, k, dc):
        r = l * 6 + k
        return self.gcols.ap[:, r, dc:dc + 1]

    def load_x(self):
        P = self.P
        S = self.S
        self.stage_reset()
        xin = self.ring(2, [D], F32)
        stg = self.ring(3, [4, 128], F32)
        k = 0
        for tt in range(S // 128):
            xt = xin[tt % 2]
            P.dma("sp", xt.ap, self.x_in[tt * 128:(tt + 1) * 128, :], writes=[xt.buf])
            for q in range(4):
                pa, pbuf = self.bank(k % 8)
                for j in range(4):
                    dc = 4 * q + j
                    P.op("pe", lambda e, pa=pa, j=j, dc=dc, xt=xt: e.transpose(
                        pa[:, j * 128:(j + 1) * 128], xt.ap[:, dc * 128:(dc + 1) * 128], self.ident32.ap),
                        [xt.buf, self.ident32.buf], pbuf, signal=(j == 3))
                sg = stg[k % 3]
                eng = "act" if k % 2 == 0 else "dve"
                if eng == "act":
                    P.op("act", lambda e, sg=sg, pa=pa: e.activation(out=sg.ap.rearrange("p a b -> p (a b)"), in_=pa, func=AF.Copy),
                         pbuf, [sg.buf])
                else:
                    P.op("dve", lambda e, sg=sg, pa=pa: e.tensor_copy(out=sg.ap.rearrange("p a b -> p (a b)"), in_=pa),
                         pbuf, [sg.buf])
                P.dma("sp", self.xT[4 * q:4 * q + 4, :, tt * 128:(tt + 1) * 128].rearrange("c p t -> p c t"), sg.ap,
                      reads=[sg.buf], writes=[self.db("xT", dc_, tt // 4) for dc_ in range(4 * q, 4 * q + 4)])
                k += 1

    def store_out(self):
        P = self.P
        S = self.S
        self.stage_reset()
        xs = self.ring(2, [DC, 512], F32)
        ot = self.ring(2, [D], F32)
        k = 0
        for tb in range(S // 512):
            x = xs[tb % 2]
            P.dma("sp", x.ap, self.xT[:, :, tb * 512:(tb + 1) * 512].rearrange("c p t -> p c t"),
                  reads=[self.db("xT", dc, tb) for dc in range(DC)], writes=[x.buf])
            for t4 in range(4):
                o = ot[(tb * 4 + t4) % 2]
                for q in range(4):
                    pa, pbuf = self.bank(k % 8)
                    for j in range(4):
                        dc = 4 * q + j
                        P.op("pe", lambda e, pa=pa, j=j, dc=dc, x=x, t4=t4: e.transpose(
                            pa[:, j * 128:(j + 1) * 128], x.ap[:, dc, t4 * 128:(t4 + 1) * 128], self.ident32.ap),
                            [x.buf, self.ident32.buf], pbuf, signal=(j == 3))
                    if k % 2 == 0:
                        P.op("act", lambda e, o=o, pa=pa, q=q: e.activation(out=o.ap[:, q * 512:(q + 1) * 512], in_=pa, func=AF.Copy),
                             pbuf, [o.buf])
                    else:
                        P.op("dve", lambda e, o=o, pa=pa, q=q: e.tensor_copy(out=o.ap[:, q * 512:(q + 1) * 512], in_=pa),
                             pbuf, [o.buf])
                    k += 1
                r0 = tb * 512 + t4 * 128
                P.dma("sp", self.out[r0:r0 + 128, :], o.ap, reads=[o.buf], writes=[self.db("out", r0)])

    def rstd_from(self, pa, pbufs, rstd, n):
        P = self.P
        P.op("act", lambda e: e.activation(out=rstd.ap, in_=pa, func=AF.Sqrt, bias=self.epsc.ap[:, 0:1], scale=1.0 / n),
             list(pbufs) + [self.epsc.buf], [rstd.buf])
        P.op("dve", lambda e: e.reciprocal(out=rstd.ap, in_=rstd.ap), [rstd.buf], [rstd.buf])

    def residual(self, l, k, t0, TS, rstd, tmp):
        P = self.P
        xr = tmp[0:3]
        fr = tmp[3:6]
        nb = TS // 512
        for dc in range(DC):
            x = xr[dc % 3]
            f = fr[dc % 3]
            xb = [self.db("xT", dc, t0 // 512 + i) for i in range(nb)]
            fb = [self.db("fT", dc, t0 // 512 + i) for i in range(nb)]
            P.dma("sp", x.ap, self.xT[dc, :, t0:t0 + TS], reads=xb, writes=[x.buf])
            P.dma("sp", f.ap, self.fT[dc, :, t0:t0 + TS], reads=fb, writes=[f.buf])
            P.op("dve", lambda e, f=f, dc=dc: e.scalar_tensor_tensor(out=f.ap, in0=f.ap, scalar=self.gcol(l, k, dc), in1=rstd.ap,
                                                                    op0=ALU.mult, op1=ALU.mult),
                 [f.buf, rstd.buf, self.gcols.buf], [f.buf])
            P.op("pool", lambda e, f=f, x=x: e.tensor_tensor(out=x.ap, in0=f.ap, in1=x.ap, op=ALU.add),
                 [f.buf, x.buf], [x.buf])
            P.dma("sp", self.xT[dc, :, t0:t0 + TS], x.ap, reads=[x.buf], writes=xb)

    def prenorm(self, l, k, t0, TS, hT, tmp):
        P = self.P
        nb = TS // 512
        xr = tmp[0:3]
        sq = tmp[3:5]
        rstd = tmp[5]
        pa, pbufs = self.bank(0, nb)
        for dc in range(DC):
            x = xr[dc % 3]
            s = sq[dc % 2]
            xb = [self.db("xT", dc, t0 // 512 + i) for i in range(nb)]
            P.dma("sp", x.ap, self.xT[dc, :, t0:t0 + TS], reads=xb, writes=[x.buf])
            P.op("act", lambda e, x=x, s=s: e.activation(out=s.ap, in_=x.ap, func=AF.Square), [x.buf], [s.buf])
            for i in range(nb):
                P.op("pe", lambda e, s=s, i=i, dc=dc: e.matmul(pa[:, i * 512:(i + 1) * 512], self.ones32.ap,
                                                              s.ap[:, i * 512:(i + 1) * 512], start=(dc == 0), stop=(dc == DC - 1)),
                     [s.buf, self.ones32.buf], [pbufs[i]], signal=(dc == DC - 1))
        self.rstd_from(pa, pbufs, rstd, D)
        for dc in range(DC):
            x = xr[(dc + 1) % 3]
            xb = [self.db("xT", dc, t0 // 512 + i) for i in range(nb)]
            P.dma("sp", x.ap, self.xT[dc, :, t0:t0 + TS], reads=xb, writes=[x.buf])
            P.op("dve", lambda e, x=x, dc=dc: e.scalar_tensor_tensor(out=hT.ap[:, dc, :], in0=x.ap, scalar=self.gcol(l, k, dc),
                                                                    in1=rstd.ap, op0=ALU.mult, op1=ALU.mult),
                 [x.buf, rstd.buf, self.gcols.buf], [hT.buf])

    def ffn(self, l, i):
        P = self.P
        S = self.S
        TS = min(1024, S)
        nb = TS // 512
        wg = self.w_gate[l, i]
        wu = self.w_up[l, i]
        wd = self.w_down[l, i]
        for sb in range(S // TS):
            t0 = sb * TS
            self.stage_reset()
            hT = self.alloc([DC, TS], BF16)
            AT = self.alloc([NFT, TS], BF16)
            wr = self.ring(3, [NFT * 128], BF16)
            tmp = self.ring(7, [TS], F32)
            self.prenorm(l, 0 if i == 0 else 4, t0, TS, hT, tmp)
            sgr = tmp[0:2]
            for ft in range(NFT):
                w = wr[ft % 3]
                wv = w.ap[:, 0:2 * DC * 128].rearrange("p (g c k) -> p g c k", g=2, c=DC)
                P.dma("pool", wv[:, 0], wg[:, ft * 128:(ft + 1) * 128].rearrange("(c p) k -> p c k", p=128), writes=[w.buf])
                P.dma("pool", wv[:, 1], wu[:, ft * 128:(ft + 1) * 128].rearrange("(c p) k -> p c k", p=128), writes=[w.buf])
                b0 = (ft % 2) * 4
                ga, gb = self.bank(b0, nb)
                ua, ub = self.bank(b0 + 2, nb)
                for (pa, pbs, g) in ((ga, gb, 0), (ua, ub, 1)):
                    for n in range(nb):
                        for dc in range(DC):
                            P.op("pe", lambda e, pa=pa, n=n, dc=dc, wv=wv, g=g: e.matmul(
                                pa[:, n * 512:(n + 1) * 512], wv[:, g, dc, :], hT.ap[:, dc, n * 512:(n + 1) * 512],
                                start=(dc == 0), stop=(dc == DC - 1)),
                                [w.buf, hT.buf], [pbs[n]], signal=(dc == DC - 1))
                sg = sgr[ft % 2]
                P.op("act", lambda e, sg=sg, ga=ga: e.activation(out=sg.ap, in_=ga, func=AF.Silu), gb, [sg.buf])
                P.op("dve", lambda e, sg=sg, ua=ua, ft=ft: e.tensor_tensor(out=AT.ap[:, ft, :], in0=sg.ap, in1=ua, op=ALU.mult),
                     [sg.buf] + list(ub), [AT.buf])
            fr = tmp[2:5]
            sq = tmp[5:7]
            sa, sbufs = self.bank(6, nb)
            for dt in range(DC):
                w = wr[(NFT + dt) % 3]
                wv = w.ap[:, 0:NFT * 128].rearrange("p (f k) -> p f k", k=128)
                P.dma("pool", wv, wd[:, dt * 128:(dt + 1) * 128].rearrange("(f p) k -> p f k", p=128), writes=[w.buf])
                fa, fb = self.bank((dt % 3) * 2, nb)
                for n in range(nb):
                    for ft in range(NFT):
                        P.op("pe", lambda e, fa=fa, n=n, ft=ft, wv=wv: e.matmul(
                            fa[:, n * 512:(n + 1) * 512], wv[:, ft, :], AT.ap[:, ft, n * 512:(n + 1) * 512],
                            start=(ft == 0), stop=(ft == NFT - 1)),
                            [w.buf, AT.buf], [fb[n]], signal=(ft == NFT - 1))
                f = fr[dt % 3]
                s = sq[dt % 2]
                P.op("act", lambda e, f=f, fa=fa: e.activation(out=f.ap, in_=fa, func=AF.Copy), fb, [f.buf])
                P.op("act", lambda e, f=f, s=s: e.activation(out=s.ap, in_=f.ap, func=AF.Square), [f.buf], [s.buf])
                for n in range(nb):
                    P.op("pe", lambda e, s=s, n=n, dt=dt: e.matmul(sa[:, n * 512:(n + 1) * 512], self.ones32.ap,
                                                                  s.ap[:, n * 512:(n + 1) * 512], start=(dt == 0), stop=(dt == DC - 1)),
                         [s.buf, self.ones32.buf], [sbufs[n]], signal=(dt == DC - 1))
                P.dma("sp", self.fT[dt, :, t0:t0 + TS], f.ap, reads=[f.buf],
                      writes=[self.db("fT", dt, t0 // 512 + j) for j in range(nb)])
            rstd = tmp[6]
            self.rstd_from(sa, sbufs, rstd, D)
            P.op("dve", lambda e: e.tensor_scalar(out=rstd.ap, in0=rstd.ap, scalar1=0.5, scalar2=None, op0=ALU.mult),
                 [rstd.buf], [rstd.buf])
            self.residual(l, 1 if i == 0 else 5, t0, TS, rstd, tmp)

    def build(self):
        self.load_x()
        if self.stages in ("all", "mix", "mla"):
            self.rope_tables()
        for l in range(self.depth):
            if "ffn" in self.stages or self.stages == "all":
                self.ffn(l, 0)
            if self.stages in ("all", "mix", "sg", "mla", "dn"):
                self.mixer(l)
            if "ffn" in self.stages or self.stages == "all":
                self.ffn(l, 1)
        self.store_out()
        self.P.emit()
        return self.nc

    def sub_reset(self):
        self.barrier()
        self.off = self.mark

    def wtile(self, dst_ap, src_rows_ap, buf, reads=()):
        self.P.dma("pool", dst_ap, src_rows_ap.rearrange("(c p) k -> p c k", p=128), reads=list(reads), writes=[buf])

    def proj_fm(self, hT, wt, ncol, S, consume, banks=(0, 1, 2, 3), K=DC):
        P = self.P
        for tb in range(S // 512):
            pa, pbs = self.bank(banks[tb % len(banks)])
            pa = pa[0:ncol, :]
            for dc in range(K):
                P.op("pe", lambda e, pa=pa, dc=dc, tb=tb: e.matmul(pa, wt.ap[:, dc, 0:ncol], hT.ap[:, dc, tb * 512:(tb + 1) * 512],
                                                               start=(dc == 0), stop=(dc == K - 1)),
                     [wt.buf, hT.buf], pbs, signal=(dc == K - 1))
            consume(tb, pa, pbs)

    def mixer(self, l):
        S = self.S
        self.stage_reset()
        hT = self.alloc([DC, S], BF16)
        self.mark = self.off
        tmp = self.ring(6, [S], F32)
        self.prenorm(l, 2, 0, S, hT, tmp)
        self.sub_reset()
        if self.stages in ("all", "mix", "sg"):
            self.sg_stage(l, hT)
            self.sub_reset()
        if self.stages in ("all", "mix", "mla"):
            self.mla_proj(l, hT)
            self.sub_reset()
        if self.stages in ("all", "mix", "dn"):
            self.dn_proj(l, hT)
        if self.stages in ("all", "mix", "mla"):
            self.mla_attn(l)
        if self.stages in ("all", "mix", "dn"):
            self.dn_core(l)
        if self.stages in ("all", "mix"):
            self.merge(l)

    def sg_stage(self, l, hT):
        P = self.P
        S = self.S
        W = self.w_in[l]
        wu = self.alloc([DC, 1024], BF16)
        wv = self.alloc([DC, 1024], BF16)
        self.wtile(wu.ap, W[:, O_SG:O_SG + 1024], wu.buf)
        self.wtile(wv.ap, W[:, O_SG + 1024:O_SG + 2048], wv.buf)
        lng = self.alloc([1024], F32)
        lnb = self.alloc([1024], F32)
        bias = self.alloc([8, 128], F32)
        P.dma("sp", lng.ap, self.ln_g[l:l + 1, :].broadcast_to([128, 1024]), writes=[lng.buf])
        P.dma("sp", lnb.ap, self.ln_b[l:l + 1, :].broadcast_to([128, 1024]), writes=[lnb.buf])
        P.dma("sp", bias.ap.rearrange("p g i -> p (g i)"),
              self.sg_b[l:l + 1].rearrange("o g i -> o (g i)").broadcast_to([128, 1024]), writes=[bias.buf])
        wraw = self.alloc([8, 128], F32)
        P.dma("sp", wraw.ap, self.sg_w[l].rearrange("g i j -> i g j"), writes=[wraw.buf])
        smask = self.alloc([128], F32)
        P.dma("sp", smask.ap, self.cst[:, 770:898], writes=[smask.buf])
        eps5 = self.alloc([1], F32)
        P.op("pool", lambda e: e.memset(eps5.ap, 1e-5), [], [eps5.buf])
        wmT = self.alloc([8, 128], BF16)
        for g in range(8):
            pa, pbs = self.bank(g // 4)
            P.op("pe", lambda e, pa=pa, g=g: e.transpose(pa[:, (g % 4) * 128:(g % 4 + 1) * 128], wraw.ap[:, g, :], self.ident32.ap),
                 [wraw.buf, self.ident32.buf], pbs)
            P.op("dve", lambda e, pa=pa, g=g: e.tensor_tensor(out=wmT.ap[:, g, :], in0=pa[:, (g % 4) * 128:(g % 4 + 1) * 128],
                                                             in1=smask.ap, op=ALU.mult),
                 list(pbs) + [smask.buf], [wmT.buf])
        uTr = self.ring(1, [8, 512], F32)
        obr = self.ring(1, [8, 512], BF16)
        vgr = self.ring(2, [1024], F32)
        vlr = self.ring(2, [1024], BF16)
        tmr = self.ring(1, [8, 128], F32)
        st = self.alloc([12], F32)
        mv = self.alloc([2], F32)
        rs = self.alloc([1], F32)
        kk = 0
        for tb in range(S // 512):
            uT = uTr[0]
            ob = obr[0]
            for g in range(8):
                pa, pbs = self.bank(g % 2)
                for dc in range(DC):
                    P.op("pe", lambda e, pa=pa, dc=dc, g=g, tb=tb: e.matmul(pa, wu.ap[:, dc, g * 128:(g + 1) * 128],
                                                                        hT.ap[:, dc, tb * 512:(tb + 1) * 512],
                                                                        start=(dc == 0), stop=(dc == DC - 1)),
                         [wu.buf, hT.buf], pbs, signal=(dc == DC - 1))
                P.op("act", lambda e, pa=pa, g=g, uT=uT: e.activation(out=uT.ap[:, g, :], in_=pa, func=AF.Gelu), pbs, [uT.buf])
            for t4 in range(4):
                tt = tb * 4 + t4
                b0 = 2 + 2 * (kk % 2)
                va, vbs = self.bank(b0, 2)
                for half in range(2):
                    for dc in range(DC):
                        P.op("pe", lambda e, va=va, dc=dc, half=half, tt=tt: e.matmul(
                            va[:, half * 512:(half + 1) * 512], hT.ap[:, dc, tt * 128:(tt + 1) * 128],
                            wv.ap[:, dc, half * 512:(half + 1) * 512], start=(dc == 0), stop=(dc == DC - 1)),
                            [wv.buf, hT.buf], [vbs[half]], signal=(dc == DC - 1))
                vg = vgr[kk % 2]
                vl = vlr[kk % 2]
                tm = tmr[0]
                P.op("act", lambda e, vg=vg, va=va: e.activation(out=vg.ap, in_=va, func=AF.Gelu), vbs, [vg.buf])
                P.op("dve", lambda e, vg=vg: e.bn_stats(out=st.ap[:, 0:6], in_=vg.ap[:, 0:512]), [vg.buf], [st.buf])
                P.op("dve", lambda e, vg=vg: e.bn_stats(out=st.ap[:, 6:12], in_=vg.ap[:, 512:1024]), [vg.buf], [st.buf])
                P.op("dve", lambda e: e.bn_aggr(out=mv.ap, in_=st.ap), [st.buf], [mv.buf])
                P.op("act", lambda e: e.activation(out=rs.ap, in_=mv.ap[:, 1:2], func=AF.Sqrt, bias=eps5.ap[:, 0:1], scale=1.0),
                     [mv.buf, eps5.buf], [rs.buf])
                P.op("dve", lambda e: e.reciprocal(out=rs.ap, in_=rs.ap), [rs.buf], [rs.buf])
                P.op("dve", lambda e, vg=vg: e.tensor_scalar(out=vg.ap, in0=vg.ap, scalar1=mv.ap[:, 0:1], scalar2=rs.ap[:, 0:1],
                                                            op0=ALU.subtract, op1=ALU.mult), [vg.buf, mv.buf, rs.buf], [vg.buf])
                P.op("pool", lambda e, vg=vg: e.tensor_tensor(out=vg.ap, in0=vg.ap, in1=lng.ap, op=ALU.mult), [vg.buf, lng.buf], [vg.buf])
                P.op("pool", lambda e, vg=vg, vl=vl: e.tensor_tensor(out=vl.ap, in0=vg.ap, in1=lnb.ap, op=ALU.add),
                     [vg.buf, lnb.buf], [vl.buf])
                ma, mbs = self.bank(6, 2)
                for g in range(8):
                    P.op("pe", lambda e, g=g, vl=vl: e.matmul(ma[:, g * 128:(g + 1) * 128], vl.ap[:, g * 128:(g + 1) * 128], wmT.ap[:, g, :],
                                                            start=True, stop=True),
                         [vl.buf, wmT.buf], [mbs[g // 4]], signal=(g % 4 == 3))
                P.op("dve", lambda e, tm=tm: e.tensor_tensor(out=tm.ap, in0=ma.rearrange("p (g i) -> p g i", i=128), in1=bias.ap, op=ALU.add),
                     list(mbs) + [bias.buf], [tm.buf])
                P.op("pool", lambda e, tm=tm, ob=ob, uT=uT, t4=t4: e.tensor_tensor(
                    out=ob.ap[:, :, t4 * 128:(t4 + 1) * 128], in0=tm.ap, in1=uT.ap[:, :, t4 * 128:(t4 + 1) * 128], op=ALU.mult),
                    [tm.buf, uT.buf], [ob.buf])
                kk += 1
            P.dma("sp", self.obT[1, :, :, tb * 512:(tb + 1) * 512].rearrange("g p t -> p g t"), ob.ap, reads=[ob.buf],
                  writes=[self.db("obT", 1, tb)])

    def merge(self, l):
        P = self.P
        S = self.S
        TS = min(1024, S)
        nb = TS // 512
        W = self.w_in[l]
        for sb in range(S // TS):
            t0 = sb * TS
            self.stage_reset()
            hT = self.alloc([DC, TS], BF16)
            mT = self.alloc([DC, TS], BF16)
            ob = self.alloc([24, TS], BF16)
            tmp = self.ring(7, [TS], F32)
            wr = self.ring(4, [DC * 128], BF16)
            self.prenorm(l, 2, t0, TS, hT, tmp)
            P.dma("sp", ob.ap, self.obT[:, :, :, t0:t0 + TS].rearrange("n g p t -> p (n g) t"),
                  reads=[self.db("obT", n, t0 // 512 + j) for n in range(3) for j in range(nb)], writes=[ob.buf])
            wk = 0
            for dt in range(DC):
                acc = tmp[dt % 2]
                for n in range(3):
                    wg = wr[wk % 4]; wk += 1
                    wgv = wg.ap.rearrange("p (c k) -> p c k", k=128)
                    c0 = O_GATE + n * D + dt * 128
                    self.wtile(wgv, W[:, c0:c0 + 128], wg.buf)
                    wb = wr[wk % 4]; wk += 1
                    wbv = wb.ap[:, 0:8 * 128].rearrange("p (c k) -> p c k", k=128)
                    self.wtile(wbv, self.w_branch[l, n][:, dt * 128:(dt + 1) * 128], wb.buf)
                    ga, gbs = self.bank(0 if n % 2 == 0 else 4, nb)
                    ya, ybs = self.bank(2 if n % 2 == 0 else 6, nb)
                    for j in range(nb):
                        for dc in range(DC):
                            P.op("pe", lambda e, ga=ga, j=j, dc=dc, wgv=wgv: e.matmul(ga[:, j * 512:(j + 1) * 512], wgv[:, dc, :],
                                                                                  hT.ap[:, dc, j * 512:(j + 1) * 512],
                                                                                  start=(dc == 0), stop=(dc == DC - 1)),
                                 [wg.buf, hT.buf], [gbs[j]], signal=(dc == DC - 1))
                    for j in range(nb):
                        for wc in range(8):
                            P.op("pe", lambda e, ya=ya, j=j, wc=wc, wbv=wbv, n=n: e.matmul(ya[:, j * 512:(j + 1) * 512], wbv[:, wc, :],
                                                                                       ob.ap[:, n * 8 + wc, j * 512:(j + 1) * 512],
                                                                                       start=(wc == 0), stop=(wc == 7)),
                                 [wb.buf, ob.buf], [ybs[j]], signal=(wc == 7))
                    gt = tmp[2 + n % 2]
                    P.op("act", lambda e, gt=gt, ga=ga: e.activation(out=gt.ap, in_=ga, func=AF.Sigmoid), gbs, [gt.buf])
                    if n == 0:
                        P.op("dve", lambda e, gt=gt, ya=ya, acc=acc: e.tensor_tensor(out=acc.ap, in0=gt.ap, in1=ya, op=ALU.mult),
                             [gt.buf] + list(ybs), [acc.buf])
                    else:
                        P.op("dve", lambda e, gt=gt, ya=ya: e.tensor_tensor(out=gt.ap, in0=gt.ap, in1=ya, op=ALU.mult),
                             [gt.buf] + list(ybs), [gt.buf])
                        if n == 1:
                            P.op("pool", lambda e, gt=gt, acc=acc: e.tensor_tensor(out=acc.ap, in0=acc.ap, in1=gt.ap, op=ALU.add),
                                 [gt.buf, acc.buf], [acc.buf])
                        else:
                            P.op("pool", lambda e, gt=gt, acc=acc, dt=dt: e.tensor_tensor(out=mT.ap[:, dt, :], in0=acc.ap, in1=gt.ap, op=ALU.add),
                                 [gt.buf, acc.buf], [mT.buf])
            fr = tmp[2:5]
            sq = tmp[5:7]
            sa, sbufs = self.bank(6, nb)
            for dt in range(DC):
                w = wr[wk % 4]; wk += 1
                wv = w.ap.rearrange("p (c k) -> p c k", k=128)
                self.wtile(wv, self.w_out[l][:, dt * 128:(dt + 1) * 128], w.buf)
                fa, fb = self.bank((dt % 3) * 2, nb)
                for j in range(nb):
                    for dc in range(DC):
                        P.op("pe", lambda e, fa=fa, j=j, dc=dc, wv=wv: e.matmul(fa[:, j * 512:(j + 1) * 512], wv[:, dc, :],
                                                                             mT.ap[:, dc, j * 512:(j + 1) * 512],
                                                                             start=(dc == 0), stop=(dc == DC - 1)),
                             [w.buf, mT.buf], [fb[j]], signal=(dc == DC - 1))
                f = fr[dt % 3]
                s = sq[dt % 2]
                P.op("act", lambda e, f=f, fa=fa: e.activation(out=f.ap, in_=fa, func=AF.Copy), fb, [f.buf])
                P.op("act", lambda e, f=f, s=s: e.activation(out=s.ap, in_=f.ap, func=AF.Square), [f.buf], [s.buf])
                for j in range(nb):
                    P.op("pe", lambda e, s=s, j=j, dt=dt: e.matmul(sa[:, j * 512:(j + 1) * 512], self.ones32.ap,
                                                                  s.ap[:, j * 512:(j + 1) * 512], start=(dt == 0), stop=(dt == DC - 1)),
                         [s.buf, self.ones32.buf], [sbufs[j]], signal=(dt == DC - 1))
                P.dma("sp", self.fT[dt, :, t0:t0 + TS], f.ap, reads=[f.buf],
                      writes=[self.db("fT", dt, t0 // 512 + j) for j in range(nb)])
            rstd = tmp[6]
            self.rstd_from(sa, sbufs, rstd, D)
            self.residual(l, 3, t0, TS, rstd, tmp)

    def cols_from_rows(self, src_ap, R, n, bank=7):
        P = self.P
        rows = self.alloc([n * 128], F32)
        P.dma("sp", rows.ap[0:R, :], src_ap, writes=[rows.buf])
        out = self.alloc([n, R], F32)
        for c0 in range(0, n, 512 // R):
            c1 = min(n, c0 + 512 // R)
            pa, pbs = self.bank(bank)
            for c in range(c0, c1):
                P.op("pe", lambda e, pa=pa, c=c, c0=c0: e.transpose(pa[:, (c - c0) * R:(c - c0 + 1) * R], rows.ap[0:R, c * 128:(c + 1) * 128],
                                                                 self.ident32.ap[0:R, 0:R]),
                     [rows.buf, self.ident32.buf], pbs)
            P.op("dve", lambda e, pa=pa, c0=c0, c1=c1: e.tensor_copy(out=out.ap[:, c0:c1, :],
                                                                    in_=pa[:, 0:(c1 - c0) * R].rearrange("p (c r) -> p c r", r=R)),
                 pbs, [out.buf])
        return out

    def rope_tables(self):
        P = self.P
        S = self.S
        self.ropeT = self.nc.dram_tensor("ropeT", [2, 64, S], F32).ap()
        self.stage_reset()
        PI = 3.14159265358979
        C1 = 6.28125
        C2 = 2 * PI - C1
        MAGIC = 12582912.0
        PIC = 3.1415925
        posi = self.alloc([S], I32)
        P.dma("sp", posi.ap[0:64, :], self.pos_in.broadcast_to([64, S]), writes=[posi.buf])
        cc = self.alloc([2], F32)
        P.dma("sp", cc.ap, self.cst[:, 768:770], writes=[cc.buf])
        ang = self.alloc([S], F32)
        k = self.alloc([S], F32)
        r = self.alloc([S], F32)
        rc = self.alloc([S], F32)
        m = self.alloc([S], F32)
        a = lambda t: t.ap[0:64, :]
        V = lambda fn, rd, wr: P.op("dve", fn, [t.buf for t in rd], [t.buf for t in wr])
        V(lambda e: e.tensor_copy(out=a(ang), in_=a(posi)), [posi], [ang])
        V(lambda e: e.tensor_scalar(out=a(ang), in0=a(ang), scalar1=cc.ap[0:64, 0:1], scalar2=None, op0=ALU.mult), [ang, cc], [ang])
        V(lambda e: e.tensor_scalar(out=a(k), in0=a(ang), scalar1=1.0 / (2 * PI), scalar2=MAGIC, op0=ALU.mult, op1=ALU.add), [ang], [k])
        V(lambda e: e.tensor_scalar(out=a(k), in0=a(k), scalar1=-MAGIC, scalar2=None, op0=ALU.add), [k], [k])
        V(lambda e: e.scalar_tensor_tensor(out=a(r), in0=a(k), scalar=-C1, in1=a(ang), op0=ALU.mult, op1=ALU.add), [k, ang], [r])
        V(lambda e: e.scalar_tensor_tensor(out=a(r), in0=a(k), scalar=-C2, in1=a(r), op0=ALU.mult, op1=ALU.add), [k, r], [r])
        V(lambda e: e.tensor_scalar(out=a(rc), in0=a(r), scalar1=PI / 2, scalar2=None, op0=ALU.add), [r], [rc])
        V(lambda e: e.tensor_scalar(out=a(m), in0=a(rc), scalar1=PI, scalar2=-2 * PI, op0=ALU.is_gt, op1=ALU.mult), [rc], [m])
        V(lambda e: e.tensor_tensor(out=a(rc), in0=a(rc), in1=a(m), op=ALU.add), [rc, m], [rc])
        for t in (r, rc):
            V(lambda e, t=t: e.tensor_scalar(out=a(t), in0=a(t), scalar1=-PIC, scalar2=PIC, op0=ALU.max, op1=ALU.min), [t], [t])
        P.op("act", lambda e: e.activation(out=a(r), in_=a(r), func=AF.Sin), [r.buf], [r.buf])
        P.op("act", lambda e: e.activation(out=a(rc), in_=a(rc), func=AF.Sin), [rc.buf], [rc.buf])
        V(lambda e: e.tensor_scalar(out=a(r), in0=a(r), scalar1=cc.ap[0:64, 1:2], scalar2=None, op0=ALU.mult), [r, cc], [r])
        P.dma("sp", self.ropeT[0], a(rc), reads=[rc.buf], writes=[self.db("rope", 0)])
        P.dma("sp", self.ropeT[1], a(r), reads=[r.buf], writes=[self.db("rope", 1)])

    def mla_proj(self, l, hT):
        P = self.P
        S = self.S
        W = self.w_in[l]
        if not hasattr(self, "cqnT"):
            self.cqnT = self.nc.dram_tensor("cqnT", [2, 4, 128, S], BF16).ap()
            self.krT = self.nc.dram_tensor("krT", [64, S], BF16).ap()
        gq = self.cols_from_rows(self.cq_g[l:l + 1, :].broadcast_to([2, 512]), 2, 4)
        gk = self.cols_from_rows(self.ckv_g[l:l + 1, :].broadcast_to([2, 512]), 2, 4)
        wr = self.ring(3, [DC * 128], BF16)
        raw = self.alloc([4, S], F32)
        o16 = self.alloc([4, S], BF16)
        sqr = self.ring(2, [S], F32)
        rstd = self.alloc([S], F32)
        nb = S // 512
        for wi, (off, g) in enumerate(((O_CQ, gq), (O_CKV, gk))):
            for rc in range(4):
                wt = wr[rc % 3]
                wt3 = T(wt.ap.rearrange("p (c k) -> p c k", k=128), wt.buf)
                self.wtile(wt3.ap, W[:, off + rc * 128:off + (rc + 1) * 128], wt.buf)

                def consume(tb, pa, pbs, rc=rc):
                    P.op("act", lambda e: e.activation(out=raw.ap[:, rc, tb * 512:(tb + 1) * 512], in_=pa, func=AF.Copy), pbs, [raw.buf])
                self.proj_fm(hT, wt3, 128, S, consume)
            sa, sbufs = self.bank(4, nb)
            for rc in range(4):
                sq = sqr[rc % 2]
                P.op("act", lambda e, sq=sq, rc=rc: e.activation(out=sq.ap, in_=raw.ap[:, rc, :], func=AF.Square), [raw.buf], [sq.buf])
                for j in range(nb):
                    P.op("pe", lambda e, sq=sq, j=j, rc=rc: e.matmul(sa[:, j * 512:(j + 1) * 512], self.ones32.ap, sq.ap[:, j * 512:(j + 1) * 512],
                                                                    start=(rc == 0), stop=(rc == 3)),
                         [sq.buf, self.ones32.buf], [sbufs[j]], signal=(rc == 3))
            self.rstd_from(sa, sbufs, rstd, 512)
            for rc in range(4):
                P.op("dve", lambda e, rc=rc, g=g: e.scalar_tensor_tensor(out=o16.ap[:, rc, :], in0=raw.ap[:, rc, :], scalar=g.ap[:, rc, 0:1],
                                                                        in1=rstd.ap, op0=ALU.mult, op1=ALU.mult),
                     [raw.buf, g.buf, rstd.buf], [o16.buf])
            P.dma("sp", self.cqnT[wi].rearrange("c p t -> p c t"), o16.ap, reads=[o16.buf], writes=[self.db("cqnT", wi)])
        wa = self.alloc([DC, 64], BF16)
        wb = self.alloc([DC, 64], BF16)
        self.wtile(wa.ap, W[:, O_KR:O_KR + 64], wa.buf)
        self.wtile(wb.ap[:, :, 0:32], W[:, O_KR + 32:O_KR + 64], wb.buf)
        self.wtile(wb.ap[:, :, 32:64], W[:, O_KR:O_KR + 32], wb.buf)
        kA = sqr[0]
        kB = sqr[1]
        CC = self.alloc([S], F32)
        SS = self.alloc([S], F32)
        P.dma("sp", CC.ap[0:64, :], self.ropeT[0], reads=[self.db("rope", 0)], writes=[CC.buf])
        P.dma("sp", SS.ap[0:64, :], self.ropeT[1], reads=[self.db("rope", 1)], writes=[SS.buf])
        for (wt, dst, tab) in ((wa, kA, CC), (wb, kB, SS)):
            def consume(tb, pa, pbs, dst=dst, tab=tab):
                P.op("dve", lambda e: e.tensor_tensor(out=dst.ap[0:64, tb * 512:(tb + 1) * 512], in0=pa, in1=tab.ap[0:64, tb * 512:(tb + 1) * 512],
                                                      op=ALU.mult), list(pbs) + [tab.buf], [dst.buf])
            self.proj_fm(hT, wt, 64, S, consume)
        k16 = self.alloc([S], BF16)
        P.op("dve", lambda e: e.tensor_tensor(out=k16.ap[0:64, :], in0=kA.ap[0:64, :], in1=kB.ap[0:64, :], op=ALU.add),
             [kA.buf, kB.buf], [k16.buf])
        P.dma("sp", self.krT, k16.ap[0:64, :], reads=[k16.buf], writes=[self.db("krT")])

    def mla_attn(self, l):
        P = self.P
        S = self.S
        NQ = S // 128
        SCALE = 192.0 ** -0.5
        self.stage_reset()
        cqn = self.alloc([4, S], BF16)
        ckvn = self.alloc([4, S], BF16)
        kr16 = self.alloc([S], BF16)
        CC = self.alloc([S], F32)
        SS = self.alloc([S], F32)
        dmask = self.alloc([128], F32)
        P.dma("sp", cqn.ap, self.cqnT[0].rearrange("c p t -> p c t"), reads=[self.db("cqnT", 0)], writes=[cqn.buf])
        P.dma("sp", ckvn.ap, self.cqnT[1].rearrange("c p t -> p c t"), reads=[self.db("cqnT", 1)], writes=[ckvn.buf])
        P.dma("sp", kr16.ap[0:64, :], self.krT, reads=[self.db("krT")], writes=[kr16.buf])
        P.dma("sp", CC.ap[0:64, :], self.ropeT[0], reads=[self.db("rope", 0)], writes=[CC.buf])
        P.dma("sp", SS.ap[0:64, :], self.ropeT[1], reads=[self.db("rope", 1)], writes=[SS.buf])
        P.dma("sp", dmask.ap, self.cst[:, 640:768], writes=[dmask.buf])
        Wkv = self.w_ukv[l]
        Wq = self.w_uq[l]
        wvv = self.alloc([4, 8, 128], BF16)
        for rc in range(4):
            P.dma("pool", wvv.ap[:, rc], Wkv[rc * 128:(rc + 1) * 128, :].rearrange("p (h t k) -> p h t k", t=2, k=128)[:, :, 1, :],
                  writes=[wvv.buf])
        v16 = self.alloc([NQ, 1024], BF16)
        for tt in range(NQ):
            va, vbs = self.bank(2 * (tt % 2), 2)
            for half in range(2):
                for rc in range(4):
                    P.op("pe", lambda e, va=va, half=half, rc=rc, tt=tt: e.matmul(
                        va[:, half * 512:(half + 1) * 512], ckvn.ap[:, rc, tt * 128:(tt + 1) * 128],
                        wvv.ap[:, rc, half * 4:(half + 1) * 4, :].rearrange("p h k -> p (h k)"), start=(rc == 0), stop=(rc == 3)),
                        [ckvn.buf, wvv.buf], [vbs[half]], signal=(rc == 3))
            P.op("act", lambda e, va=va, tt=tt: e.activation(out=v16.ap[:, tt, :], in_=va, func=AF.Copy), vbs, [v16.buf])
        wqr = self.ring(2, [4, 192], BF16)
        wqbr = self.ring(2, [4, 64], BF16)
        wkr = self.ring(2, [4, 128], BF16)
        qn16 = self.alloc([S], BF16)
        kn16 = self.alloc([S], BF16)
        qr16 = self.alloc([S], BF16)
        qA = self.alloc([S], F32)
        qB = self.alloc([S], F32)
        P16 = self.alloc([S], BF16)
        PT16 = self.alloc([16, 128], BF16)
        o16 = self.alloc([128], BF16)
        oT16r = self.ring(2, [512], BF16)
        mx = self.alloc([1], F32)
        nmx = self.alloc([1], F32)
        rsum = self.alloc([1], F32)
        rinv = self.alloc([1], F32)
        ps16 = self.psum.bitcast(BF16)
        for h in range(8):
            wq = wqr[h % 2]
            wqb = wqbr[h % 2]
            wk = wkr[h % 2]
            self.wtile(wq.ap, Wq[:, h * 192:(h + 1) * 192], wq.buf)
            self.wtile(wqb.ap[:, :, 0:32], Wq[:, h * 192 + 160:h * 192 + 192], wqb.buf)
            self.wtile(wqb.ap[:, :, 32:64], Wq[:, h * 192 + 128:h * 192 + 160], wqb.buf)
            self.wtile(wk.ap, Wkv[:, h * 256:h * 256 + 128], wk.buf)

            def c_qn(tb, pa, pbs):
                P.op("act", lambda e: e.activation(out=qn16.ap[:, tb * 512:(tb + 1) * 512], in_=pa, func=AF.Copy, scale=SCALE), pbs, [qn16.buf])

            def c_kn(tb, pa, pbs):
                P.op("act", lambda e: e.activation(out=kn16.ap[:, tb * 512:(tb + 1) * 512], in_=pa, func=AF.Copy), pbs, [kn16.buf])

            def c_qa(tb, pa, pbs):
                P.op("dve", lambda e: e.tensor_tensor(out=qA.ap[0:64, tb * 512:(tb + 1) * 512], in0=pa, in1=CC.ap[0:64, tb * 512:(tb + 1) * 512],
                                                      op=ALU.mult), list(pbs) + [CC.buf], [qA.buf])

            def c_qb(tb, pa, pbs):
                P.op("dve", lambda e: e.tensor_tensor(out=qB.ap[0:64, tb * 512:(tb + 1) * 512], in0=pa, in1=SS.ap[0:64, tb * 512:(tb + 1) * 512],
                                                      op=ALU.mult), list(pbs) + [SS.buf], [qB.buf])
            self.proj_fm(cqn, T(wq.ap[:, :, 0:128], wq.buf), 128, S, c_qn, K=4)
            self.proj_fm(ckvn, wk, 128, S, c_kn, K=4)
            self.proj_fm(cqn, T(wq.ap[:, :, 128:192], wq.buf), 64, S, c_qa, K=4)
            self.proj_fm(cqn, wqb, 64, S, c_qb, K=4)
            P.op("dve", lambda e: e.tensor_tensor(out=qA.ap[0:64, :], in0=qA.ap[0:64, :], in1=qB.ap[0:64, :], op=ALU.add),
                 [qA.buf, qB.buf], [qA.buf])
            P.op("act", lambda e: e.activation(out=qr16.ap[0:64, :], in_=qA.ap[0:64, :], func=AF.Copy, scale=SCALE), [qA.buf], [qr16.buf])
            for qi in range(NQ):
                nk = (qi + 1) * 128
                nkb = (nk + 511) // 512
                sa, sbs = self.bank(0, nkb)
                q0 = qi * 128
                for kb in range(nkb):
                    w = min(512, nk - kb * 512)
                    P.op("pe", lambda e, kb=kb, w=w, q0=q0: e.matmul(sa[:, kb * 512:kb * 512 + w], qn16.ap[:, q0:q0 + 128],
                                                                    kn16.ap[:, kb * 512:kb * 512 + w], start=True, stop=False),
                         [qn16.buf, kn16.buf], [sbs[kb]], signal=False)
                    P.op("pe", lambda e, kb=kb, w=w, q0=q0: e.matmul(sa[:, kb * 512:kb * 512 + w], qr16.ap[0:64, q0:q0 + 128],
                                                                    kr16.ap[0:64, kb * 512:kb * 512 + w], start=False, stop=True),
                         [qr16.buf, kr16.buf], [sbs[kb]], signal=True)
                P.op("dve", lambda e, nk=nk: e.tensor_tensor(out=sa[:, nk - 128:nk], in0=sa[:, nk - 128:nk], in1=dmask.ap, op=ALU.add),
                     [sbs[nkb - 1], dmask.buf], [sbs[nkb - 1]])
                P.op("dve", lambda e, nk=nk: e.reduce_max(out=mx.ap, in_=sa[:, 0:nk], axis=AX.X), list(sbs), [mx.buf])
                P.op("dve", lambda e: e.tensor_scalar(out=nmx.ap, in0=mx.ap, scalar1=-1.0, scalar2=None, op0=ALU.mult), [mx.buf], [nmx.buf])
                P.op("act", lambda e, nk=nk: e.activation(out=P16.ap[:, 0:nk], in_=sa[:, 0:nk], func=AF.Exp, bias=nmx.ap[:, 0:1], scale=1.0,
                                                         accum_out=rsum.ap[:, 0:1]), list(sbs) + [nmx.buf], [P16.buf, rsum.buf])
                P.op("dve", lambda e: e.reciprocal(out=rinv.ap, in_=rsum.ap), [rsum.buf], [rinv.buf])
                nblk = nk // 128
                for kb in range(nblk):
                    bk = 4 + kb // 8
                    P.op("pe", lambda e, kb=kb, bk=bk: e.transpose(ps16[:, bk * 1024 + (kb % 8) * 128:bk * 1024 + (kb % 8 + 1) * 128],
                                                                  P16.ap[:, kb * 128:(kb + 1) * 128], self.ident16.ap),
                         [P16.buf, self.ident16.buf], [self.pb[bk]], signal=(kb % 8 == 7 or kb == nblk - 1))
                for bi in range((nblk + 7) // 8):
                    n8 = min(8, nblk - bi * 8)
                    src = ps16[:, (4 + bi) * 1024:(4 + bi) * 1024 + n8 * 128]
                    dst = PT16.ap[:, bi * 8:bi * 8 + n8, :].rearrange("p a b -> p (a b)")
                    if bi == 0:
                        P.op("act", lambda e, src=src, dst=dst: e.activation(out=dst, in_=src, func=AF.Copy), [self.pb[4 + bi]], [PT16.buf])
                    else:
                        P.op("dve", lambda e, src=src, dst=dst: e.tensor_copy(out=dst, in_=src), [self.pb[4 + bi]], [PT16.buf])
                oa, obs = self.bank(6)
                for kb in range(nblk):
                    P.op("pe", lambda e, kb=kb, h=h: e.matmul(oa[:, 0:128], PT16.ap[:, kb, :], v16.ap[:, kb, h * 128:(h + 1) * 128],
                                                             start=(kb == 0), stop=(kb == nblk - 1)),
                         [PT16.buf, v16.buf], obs, signal=(kb == nblk - 1))
                P.op("act", lambda e: e.activation(out=o16.ap, in_=oa[:, 0:128], func=AF.Identity, scale=rinv.ap[:, 0:1]),
                     list(obs) + [rinv.buf], [o16.buf])
                P.op("pe", lambda e, qi=qi: e.transpose(ps16[:, 7 * 1024 + (qi % 4) * 128:7 * 1024 + (qi % 4 + 1) * 128], o16.ap, self.ident16.ap),
                     [o16.buf, self.ident16.buf], [self.pb[7]])
                if qi % 4 == 3:
                    oT = oT16r[(qi // 4) % 2]
                    P.op("dve", lambda e, oT=oT: e.tensor_copy(out=oT.ap, in_=ps16[:, 7 * 1024:7 * 1024 + 512]), [self.pb[7]], [oT.buf])
                    P.dma("sp", self.obT[2, h, :, (qi // 4) * 512:(qi // 4 + 1) * 512], oT.ap, reads=[oT.buf],
                          writes=[self.db("obT", 2, qi // 4)])

    def dn_proj(self, l, hT):
        P = self.P
        S = self.S
        NT = S // 128
        W = self.w_in[l]
        if not hasattr(self, "qkvzT"):
            self.qkvzT = self.nc.dram_tensor("qkvzT", [32, 128, S], F32).ap()
            self.gbD = self.nc.dram_tensor("gbD", [128, NT * 16], F32).ap()
        wr = self.ring(3, [DC * 128], BF16)
        outr = self.ring(3, [S], F32)
        for t in range(32):
            wt = wr[t % 3]
            wt3 = T(wt.ap.rearrange("p (c k) -> p c k", k=128), wt.buf)
            self.wtile(wt3.ap, W[:, t * 128:(t + 1) * 128], wt.buf)
            o = outr[t % 3]

            def consume(tb, pa, pbs, o=o, t=t):
                P.op("act", lambda e: e.activation(out=o.ap[:, tb * 512:(tb + 1) * 512], in_=pa, func=(AF.Silu if t >= 24 else AF.Copy)),
                     pbs, [o.buf])
            self.proj_fm(hT, wt3, 128, S, consume)
            P.dma("sp", self.qkvzT[t], o.ap, reads=[o.buf], writes=[self.db("qkvz", t)])
        wab = self.alloc([DC, 16], BF16)
        self.wtile(wab.ap, W[:, O_A:O_A + 16], wab.buf)
        pa, pbs = self.bank(4)
        for tt in range(NT):
            for dc in range(DC):
                P.op("pe", lambda e, tt=tt, dc=dc: e.matmul(pa[:, tt * 16:(tt + 1) * 16], hT.ap[:, dc, tt * 128:(tt + 1) * 128], wab.ap[:, dc, :],
                                                           start=(dc == 0), stop=(dc == DC - 1)),
                     [hT.buf, wab.buf], pbs, signal=(dc == DC - 1))
        ab = pa[:, 0:NT * 16].rearrange("p (t k) -> p t k", k=16)
        dtb = self.alloc([8], F32)
        alg = self.alloc([8], F32)
        P.dma("sp", dtb.ap, self.dt_bias[l:l + 1, :].broadcast_to([128, 8]), writes=[dtb.buf])
        P.dma("sp", alg.ap, self.a_log[l:l + 1, :].broadcast_to([128, 8]), writes=[alg.buf])
        P.op("act", lambda e: e.activation(out=alg.ap, in_=alg.ap, func=AF.Exp), [alg.buf], [alg.buf])
        bc = lambda t: t.ap.unsqueeze(1).broadcast_to([128, NT, 8])
        x = self.alloc([NT, 8], F32)
        ax = self.alloc([NT, 8], F32)
        gb = self.alloc([NT, 16], F32)
        P.op("dve", lambda e: e.tensor_tensor(out=x.ap, in0=ab[:, :, 0:8], in1=bc(dtb), op=ALU.add), list(pbs) + [dtb.buf], [x.buf])
        P.op("dve", lambda e: e.scalar_tensor_tensor(out=ax.ap, in0=x.ap, scalar=-1.0, in1=x.ap, op0=ALU.mult, op1=ALU.max), [x.buf], [ax.buf])
        P.op("act", lambda e: e.activation(out=ax.ap, in_=ax.ap, func=AF.Exp, scale=-1.0), [ax.buf], [ax.buf])
        P.op("act", lambda e: e.activation(out=ax.ap, in_=ax.ap, func=AF.Ln, bias=self.ones32.ap[:, 0:1], scale=1.0), [ax.buf, self.ones32.buf], [ax.buf])
        P.op("dve", lambda e: e.scalar_tensor_tensor(out=x.ap, in0=x.ap, scalar=0.0, in1=ax.ap, op0=ALU.max, op1=ALU.add), [x.buf, ax.buf], [x.buf])
        P.op("dve", lambda e: e.scalar_tensor_tensor(out=gb.ap[:, :, 0:8], in0=x.ap, scalar=-1.0, in1=bc(alg), op0=ALU.mult, op1=ALU.mult),
             [x.buf, alg.buf], [gb.buf])
        P.op("act", lambda e: e.activation(out=gb.ap[:, :, 8:16], in_=ab[:, :, 8:16], func=AF.Sigmoid), pbs, [gb.buf])
        P.dma("sp", self.gbD, gb.ap.rearrange("p t k -> p (t k)"), reads=[gb.buf], writes=[self.db("gbD")])

    def dn_core(self, l):
        P = self.P
        S = self.S
        NT = S // 128
        nb = S // 512
        self.stage_reset()
        cw = self.cols_from_rows(self.conv_w[l], 4, 24)
        dng = self.cols_from_rows(self.dn_g[l:l + 1, :].broadcast_to([2, 128]), 2, 1)
        Ms = self.alloc([128], F32)
        U = self.alloc([128], F32)
        P.dma("sp", Ms.ap, self.cst[:, 256:384], writes=[Ms.buf])
        P.dma("sp", U.ap, self.cst[:, 384:512], writes=[U.buf])
        gb = self.alloc([NT, 16], F32)
        P.dma("sp", gb.ap.rearrange("p t k -> p (t k)"), self.gbD, reads=[self.db("gbD")], writes=[gb.buf])
        I32_ = self.ident32
        ones = self.ones32
        gall = gb.ap[:, :, 0:8]
        ball = gb.ap[:, :, 8:16]
        GC = self.alloc([NT, 8], F32)
        egc = self.alloc([NT, 8], F32)
        dtl = self.alloc([NT, 8], F32)
        egl = self.alloc([NT, 8], F32)
        bge = self.alloc([NT, 8], F32)
        nbt = self.alloc([NT, 8], F32)
        pa6, pb6 = self.bank(6)
        pa7, pb7 = self.bank(7)
        v3 = lambda ap: ap.rearrange("p (t k) -> p t k", k=8)
        P.op("pe", lambda e: e.matmul(pa6[:, 0:NT * 8], U.ap, gall, start=True, stop=True), [U.buf, gb.buf], pb6)
        P.op("pe", lambda e: e.matmul(pa7[:, 0:NT * 8], ones.ap, gall, start=True, stop=True), [ones.buf, gb.buf], pb7)
        P.op("act", lambda e: e.activation(out=GC.ap, in_=v3(pa6[:, 0:NT * 8]), func=AF.Copy), pb6, [GC.buf])
        P.op("act", lambda e: e.activation(out=egc.ap, in_=v3(pa6[:, 0:NT * 8]), func=AF.Exp), pb6, [egc.buf])
        P.op("act", lambda e: e.activation(out=egl.ap, in_=v3(pa7[:, 0:NT * 8]), func=AF.Exp), pb7, [egl.buf])
        P.op("dve", lambda e: e.tensor_tensor(out=dtl.ap, in0=v3(pa7[:, 0:NT * 8]), in1=GC.ap, op=ALU.subtract), list(pb7) + [GC.buf], [dtl.buf])
        P.op("act", lambda e: e.activation(out=dtl.ap, in_=dtl.ap, func=AF.Exp), [dtl.buf], [dtl.buf])
        P.op("dve", lambda e: e.tensor_tensor(out=bge.ap, in0=egc.ap, in1=ball, op=ALU.mult), [egc.buf, gb.buf], [bge.buf])
        P.op("dve", lambda e: e.tensor_scalar(out=nbt.ap, in0=ball, scalar1=-1.0, scalar2=None, op0=ALU.mult), [gb.buf], [nbt.buf])

        xp = self.alloc([S + 4], F32)
        acc = self.alloc([S], F32)
        sq = self.alloc([S], F32)
        rs = self.alloc([S], F32)
        qT = self.alloc([S], F32)
        kT = self.alloc([S], F32)
        vT = self.alloc([S], F32)
        ktm = self.alloc([NT, 128], F32)
        vtm = self.alloc([NT, 128], F32)
        wTs = self.alloc([NT, 128], F32)
        us = self.alloc([NT, 128], F32)
        qgs = self.alloc([NT, 128], F32)
        qks = self.alloc([NT, 128], F32)
        kds = self.alloc([NT, 128], F32)
        oT = self.alloc([S], F32)
        zs = self.alloc([S], F32)
        o16 = self.alloc([S], BF16)
        St = self.alloc([128], F32)
        vnew = self.alloc([128], F32)
        sm = lambda: self.alloc([128], F32)
        NSLOT = 4
        slots = [[sm() for _ in range(13)] for _ in range(NSLOT)]
        wTs_, us_, qgs_, qks_, kds_ = wTs, us, qgs, qks, kds
        wTs, us, qgs, qks, kds = ([T(t.ap[:, c, :]) for c in range(NT)] for t in (wTs_, us_, qgs_, qks_, kds_))
        P.op("pool", lambda e: e.memset(xp.ap[:, 0:3], 0.0), [], [xp.buf])
        for h in range(8):
            for wi, dst in ((0, qT), (1, kT), (2, vT)):
                t = wi * 8 + h
                P.dma("sp", xp.ap[:, 3:3 + S], self.qkvzT[t], reads=[self.db("qkvz", t)], writes=[xp.buf])
                P.op("act", lambda e, t=t: e.activation(out=acc.ap, in_=xp.ap[:, 3:3 + S], func=AF.Identity, scale=cw.ap[:, t, 3:4]),
                     [xp.buf, cw.buf], [acc.buf])
                for k in range(3):
                    P.op("dve", lambda e, t=t, k=k: e.scalar_tensor_tensor(out=acc.ap, in0=xp.ap[:, k:k + S], scalar=cw.ap[:, t, k:k + 1],
                                                                          in1=acc.ap, op0=ALU.mult, op1=ALU.add),
                         [xp.buf, cw.buf, acc.buf], [acc.buf])
                if wi == 2:
                    P.op("act", lambda e, dst=dst: e.activation(out=dst.ap, in_=acc.ap, func=AF.Silu), [acc.buf], [dst.buf])
                    continue
                P.op("act", lambda e: e.activation(out=acc.ap, in_=acc.ap, func=AF.Silu), [acc.buf], [acc.buf])
                P.op("act", lambda e: e.activation(out=sq.ap, in_=acc.ap, func=AF.Square), [acc.buf], [sq.buf])
                sa, sbs = self.bank(0, nb)
                for j in range(nb):
                    P.op("pe", lambda e, j=j: e.matmul(sa[:, j * 512:(j + 1) * 512], ones.ap, sq.ap[:, j * 512:(j + 1) * 512], start=True, stop=True),
                         [sq.buf, ones.buf], [sbs[j]])
                self.rstd_from(sa, sbs, rs, 1)
                P.op("dve", lambda e, dst=dst, wi=wi: e.scalar_tensor_tensor(out=dst.ap, in0=acc.ap, scalar=(128.0 ** -0.5 if wi == 0 else 1.0),
                                                                            in1=rs.ap, op0=ALU.mult, op1=ALU.mult),
                     [acc.buf, rs.buf], [dst.buf])
            P.dma("sp", zs.ap, self.qkvzT[24 + h], reads=[self.db("qkvz", 24 + h)], writes=[zs.buf])
            for src, dst in ((kT, ktm), (vT, vtm)):
                for c4 in range(NT // 4):
                    pa, pbs = self.bank(c4 % 2)
                    for j in range(4):
                        c = c4 * 4 + j
                        P.op("pe", lambda e, pa=pa, j=j, c=c, src=src: e.transpose(pa[:, j * 128:(j + 1) * 128], src.ap[:, c * 128:(c + 1) * 128], I32_.ap),
                             [src.buf, I32_.buf], pbs, signal=(j == 3))
                    P.op("act", lambda e, pa=pa, c4=c4, dst=dst: e.activation(out=dst.ap[:, c4 * 4:(c4 + 1) * 4, :].rearrange("p a b -> p (a b)"),
                                                                          in_=pa, func=AF.Copy), pbs, [dst.buf])
            def chunk_body(c, sl, h=h):
                gB, nD, Xp, E, ET, EG, Nk, NTk, TT, kbg, vb, Nk2, NTk2 = slots[sl]
                cs = slice(c * 128, (c + 1) * 128)
                p0, b0 = self.bank(2 * sl)
                p1, b1 = self.bank(2 * sl + 1)
                KK, KQ, GR, NTp = p0[:, 0:128], p0[:, 128:256], p0[:, 256:384], p0[:, 384:512]
                P.op("pe", lambda e: e.matmul(KK, kT.ap[:, cs], kT.ap[:, cs], start=True, stop=True), [kT.buf], b0)
                P.op("pe", lambda e: e.matmul(KQ, kT.ap[:, cs], qT.ap[:, cs], start=True, stop=True), [kT.buf, qT.buf], b0)
                P.op("dve", lambda e: e.tensor_scalar(out=gB.ap, in0=ones.ap, scalar1=gb.ap[:, c, h:h + 1], scalar2=None, op0=ALU.mult),
                     [ones.buf, gb.buf], [gB.buf])
                P.op("pe", lambda e: e.matmul(GR, gB.ap, U.ap, start=True, stop=True), [gB.buf, U.buf], b0)
                yield
                P.op("dve", lambda e: e.tensor_scalar(out=nD.ap, in0=GR, scalar1=GC.ap[:, c, h:h + 1], scalar2=0.0,
                                                      op0=ALU.subtract, op1=ALU.max), list(b0) + [GC.buf], [nD.buf])
                P.op("dve", lambda e: e.tensor_scalar(out=Xp.ap, in0=GR, scalar1=GC.ap[:, c, h:h + 1], scalar2=0.0,
                                                      op0=ALU.subtract, op1=ALU.min), list(b0) + [GC.buf], [Xp.buf])
                P.op("act", lambda e: e.activation(out=EG.ap, in_=GR, func=AF.Exp), b0, [EG.buf])
                yield
                P.op("act", lambda e: e.activation(out=E.ap, in_=nD.ap, func=AF.Exp, scale=-1.0), [nD.buf], [E.buf])
                P.op("act", lambda e: e.activation(out=ET.ap, in_=Xp.ap, func=AF.Exp), [Xp.buf], [ET.buf])
                yield
                P.op("pool", lambda e: e.tensor_tensor(out=E.ap, in0=E.ap, in1=Ms.ap, op=ALU.mult), [E.buf, Ms.buf], [E.buf])
                P.op("pool", lambda e: e.tensor_tensor(out=ET.ap, in0=ET.ap, in1=U.ap, op=ALU.mult), [ET.buf, U.buf], [ET.buf])
                P.op("pool", lambda e: e.tensor_tensor(out=qgs[c].ap, in0=qT.ap[:, cs], in1=EG.ap, op=ALU.mult),
                     [qT.buf, EG.buf], [qgs[c].buf])
                yield
                P.op("dve", lambda e: e.scalar_tensor_tensor(out=Nk.ap, in0=KK, scalar=nbt.ap[:, c, h:h + 1], in1=E.ap,
                                                             op0=ALU.mult, op1=ALU.mult), list(b0) + [nbt.buf, E.buf], [Nk.buf])
                P.op("dve", lambda e: e.tensor_tensor(out=qks[c].ap, in0=KQ, in1=ET.ap, op=ALU.mult), list(b0) + [ET.buf], [qks[c].buf])
                yield
                P.op("pe", lambda e: e.transpose(NTp, Nk.ap, I32_.ap), [Nk.buf, I32_.buf], b0)
                yield
                P.op("act", lambda e: e.activation(out=NTk.ap, in_=NTp, func=AF.Copy), b0, [NTk.buf])
                P.op("dve", lambda e: e.tensor_tensor(out=TT.ap, in0=NTp, in1=I32_.ap, op=ALU.add), list(b0) + [I32_.buf], [TT.buf])
                yield
                cur, curT, nxt, nxtT = Nk, NTk, Nk2, NTk2
                A2, AT2, TU = p1[:, 0:128], p1[:, 128:256], p1[:, 256:384]
                for lev in range(6):
                    P.op("pe", lambda e, cur=cur, curT=curT: e.matmul(A2, curT.ap, cur.ap, start=True, stop=True), [cur.buf, curT.buf], b1)
                    if lev < 5:
                        P.op("pe", lambda e, cur=cur, curT=curT: e.matmul(AT2, cur.ap, curT.ap, start=True, stop=True),
                             [cur.buf, curT.buf], b1)
                    yield
                    P.op("act", lambda e, nxt=nxt: e.activation(out=nxt.ap, in_=A2, func=AF.Copy), b1, [nxt.buf])
                    if lev < 5:
                        P.op("dve", lambda e, nxtT=nxtT: e.tensor_copy(out=nxtT.ap, in_=AT2), b1, [nxtT.buf])
                    yield
                    P.op("pe", lambda e, nxt=nxt: e.matmul(TU, nxt.ap, TT.ap, start=True, stop=True), [nxt.buf, TT.buf], b1)
                    yield
                    P.op("dve", lambda e: e.tensor_tensor(out=TT.ap, in0=TU, in1=TT.ap, op=ALU.add), list(b1) + [TT.buf], [TT.buf])
                    cur, curT, nxt, nxtT = nxt, nxtT, cur, curT
                P.op("dve", lambda e: e.tensor_scalar(out=kbg.ap, in0=ktm.ap[:, c, :], scalar1=bge.ap[:, c, h:h + 1], scalar2=None, op0=ALU.mult),
                     [ktm.buf, bge.buf], [kbg.buf])
                P.op("pool", lambda e: e.tensor_scalar(out=vb.ap, in0=vtm.ap[:, c, :], scalar1=gb.ap[:, c, 8 + h:9 + h], scalar2=None, op0=ALU.mult),
                     [vtm.buf, gb.buf], [vb.buf])
                P.op("pool", lambda e: e.tensor_scalar(out=kds[c].ap, in0=ktm.ap[:, c, :], scalar1=dtl.ap[:, c, h:h + 1], scalar2=None,
                                                       op0=ALU.mult), [ktm.buf, dtl.buf], [kds[c].buf])
                yield
                P.op("pe", lambda e: e.matmul(p0[:, 0:128], kbg.ap, TT.ap, start=True, stop=True), [kbg.buf, TT.buf], b0)
                P.op("pe", lambda e: e.matmul(p0[:, 128:256], TT.ap, vb.ap, start=True, stop=True), [vb.buf, TT.buf], b0)
                yield
                P.op("act", lambda e: e.activation(out=wTs[c].ap, in_=p0[:, 0:128], func=AF.Copy), b0, [wTs[c].buf])
                P.op("dve", lambda e: e.tensor_copy(out=us[c].ap, in_=p0[:, 128:256]), b0, [us[c].buf])

            for c0 in range(0, NT, NSLOT):
                gens = [chunk_body(c, c - c0) for c in range(c0, min(NT, c0 + NSLOT))]
                while gens:
                    for g_ in list(gens):
                        try:
                            next(g_)
                        except StopIteration:
                            gens.remove(g_)
            P.op("pool", lambda e: e.memset(St.ap, 0.0), [], [St.buf])
            p4, b4 = self.bank(4)
            p5, b5 = self.bank(5)
            for c in range(NT):
                P.op("pe", lambda e, c=c: e.matmul(p4[:, 0:128], wTs[c].ap, St.ap, start=True, stop=True), [wTs[c].buf, St.buf], b4)
                P.op("dve", lambda e, c=c: e.tensor_tensor(out=vnew.ap, in0=us[c].ap, in1=p4[:, 0:128], op=ALU.subtract),
                     [us[c].buf] + list(b4), [vnew.buf])
                P.op("pe", lambda e, c=c: e.matmul(p5[:, 0:128], St.ap, qgs[c].ap, start=True, stop=False), [St.buf, qgs[c].buf], b5, signal=False)
                P.op("pe", lambda e, c=c: e.matmul(p5[:, 0:128], vnew.ap, qks[c].ap, start=False, stop=True), [vnew.buf, qks[c].buf], b5)
                P.op("pe", lambda e, c=c: e.matmul(p4[:, 128:256], kds[c].ap, vnew.ap, start=True, stop=True), [kds[c].buf, vnew.buf], b4)
                P.op("act", lambda e, c=c: e.activation(out=oT.ap[:, c * 128:(c + 1) * 128], in_=p5[:, 0:128], func=AF.Copy), b5, [oT.buf])
                P.op("dve", lambda e, c=c, h=h: e.scalar_tensor_tensor(out=St.ap, in0=St.ap, scalar=egl.ap[:, c, h:h + 1], in1=p4[:, 128:256],
                                                                      op0=ALU.mult, op1=ALU.add), [St.buf, egl.buf] + list(b4), [St.buf])
            P.op("act", lambda e: e.activation(out=sq.ap, in_=oT.ap, func=AF.Square), [oT.buf], [sq.buf])
            sa, sbs = self.bank(0, nb)
            for j in range(nb):
                P.op("pe", lambda e, j=j: e.matmul(sa[:, j * 512:(j + 1) * 512], ones.ap, sq.ap[:, j * 512:(j + 1) * 512], start=True, stop=True),
                     [sq.buf, ones.buf], [sbs[j]])
            self.rstd_from(sa, sbs, rs, 128)
            P.op("dve", lambda e: e.scalar_tensor_tensor(out=oT.ap, in0=oT.ap, scalar=dng.ap[:, 0, 0:1], in1=rs.ap, op0=ALU.mult, op1=ALU.mult),
                 [oT.buf, dng.buf, rs.buf], [oT.buf])
            P.op("pool", lambda e: e.tensor_tensor(out=o16.ap, in0=oT.ap, in1=zs.ap, op=ALU.mult), [oT.buf, zs.buf], [o16.buf])
            P.dma("sp", self.obT[0, h], o16.ap, reads=[o16.buf], writes=[self.db("obT", 0, j) for j in range(nb)])


def host_consts():
    c = np.zeros((128, 1024), np.float32)
    i = np.arange(128)[:, None]
    j = np.arange(128)[None, :]
    c[:, 0:128] = np.eye(128, dtype=np.float32)
    c[:, 128:256] = 1.0
    c[:, 256:384] = (i > j)
    c[:, 384:512] = (i <= j)
    c[:, 640:768] = np.where((i < 64) & (j >= 64), -30000.0, 0.0)
    inv_freq = np.power(np.float32(10000.0), -np.arange(0, 64, 2, dtype=np.float32) / np.float32(64)).astype(np.float32)
    c[0:64, 768] = np.concatenate([inv_freq, inv_freq])
    c[0:32, 769] = -1.0
    c[32:64, 769] = 1.0
    c[:, 770:898] = np.where((i >= 64) & (j < 64), 0.0, 1.0)
    return c


_CACHE = {}


def kernel(**inputs):
    x = np.asarray(inputs["x"], np.float32)
    B, S, _ = x.shape
    if "nc" not in _CACHE:
        _CACHE["nc"] = KB(S).build()
    nc = _CACHE["nc"]
    shared = {k: np.ascontiguousarray(np.asarray(v)) for k, v in inputs.items() if k not in ("x", "positions")}
    shared["consts"] = host_consts()
    pos = np.asarray(inputs["positions"], np.int32)
    in_maps = []
    for c in range(8):
        b = c % B
        m = dict(shared)
        m["x"] = np.ascontiguousarray(x[b])
        m["positions"] = np.ascontiguousarray(pos[b:b + 1])
        in_maps.append(m)
    res = run_bass_kernel_spmd(nc, in_maps, core_ids=list(range(8)))
    return np.stack([np.asarray(res.results[b]["out"], np.float32) for b in range(B)], axis=0)
```
